# Optimizing a Trainium2 kernel written in Bass

```python
import math
import jax, jax.numpy as jnp
from jax import lax
import numpy as np

D_MODEL = 2048
BATCH = 16
SEQ = 256
DEPTH = 2
DEC_BATCH = 2
DEC_SEQ = 1024
PAST_LEN = 512

GRID_W = 64
D_ATT = 1024
D_HY = 1024
D_MIX = D_ATT + D_HY
HEAD_DIM = 128
N_HEADS = D_ATT // HEAD_DIM
N_KV_HEADS = 4
GROUP = N_HEADS // N_KV_HEADS
D_KV = N_KV_HEADS * HEAD_DIM
Q_BLOCK = 128
ROPE_THETA = 10000.0
ROPE_PAIRS_AXIS = HEAD_DIM // 4
SHORT_CONV = 3
POS_BANDS = 16
POS_EMB = 1 + 2 * POS_BANDS
FILT_HID = 64
DECAY_TARGET = 1e-2
FAST_DECAY_PCT = 0.3
SLOW_DECAY_PCT = 1.5
DECAY_SHIFT = 0.05
EPS = 1e-6
SPLIT_Q = D_ATT
SPLIT_K = SPLIT_Q + D_KV
SPLIT_V = SPLIT_K + D_KV
SPLIT_GA = SPLIT_V + D_ATT
SPLIT_HY = SPLIT_GA + 3 * D_HY
D_IN = SPLIT_HY + D_HY

kernel_name = "hymba_attn_hyena_prefix_dit_step"


def rmsnorm(x, g):
    x32 = x.astype(jnp.float32)
    y = x32 * lax.rsqrt(jnp.mean(x32 * x32, axis=-1, keepdims=True) + EPS)
    return (y * g.astype(jnp.float32)).astype(x.dtype)


def rope_tables(n_tokens):
    rows = n_tokens // GRID_W
    t = jnp.arange(rows * GRID_W)
    row = (t // GRID_W).astype(jnp.float32)
    col = (t % GRID_W).astype(jnp.float32)
    inv_freq = ROPE_THETA ** (-jnp.arange(ROPE_PAIRS_AXIS, dtype=jnp.float32) / ROPE_PAIRS_AXIS)
    ang = jnp.concatenate([row[:, None] * inv_freq, col[:, None] * inv_freq], axis=-1)
    return jnp.cos(ang), jnp.sin(ang)


def apply_rope(x, cos, sin):
    xr = x.reshape(x.shape[:-1] + (HEAD_DIM // 2, 2))
    x0, x1 = xr[..., 0], xr[..., 1]
    c = cos[None, :, None, :].astype(x.dtype)
    s = sin[None, :, None, :].astype(x.dtype)
    out = jnp.stack([x0 * c - x1 * s, x0 * s + x1 * c], axis=-1)
    return out.reshape(x.shape)


def attend(q, k, v):
    B, Lq = q.shape[0], q.shape[1]
    nb = Lq // Q_BLOCK
    scale = 1.0 / math.sqrt(HEAD_DIM)
    qb = q.reshape(B, nb, Q_BLOCK, N_KV_HEADS, GROUP, HEAD_DIM).transpose(1, 0, 2, 3, 4, 5)

    def one_block(qblk):
        s = jnp.einsum('bqkgd,bskd->bkgqs', qblk, k).astype(jnp.float32) * scale
        p = jax.nn.softmax(s, axis=-1).astype(v.dtype)
        return jnp.einsum('bkgqs,bskd->bqkgd', p, v)

    o = lax.map(one_block, qb)
    return o.transpose(1, 0, 2, 3, 4, 5).reshape(B, Lq, D_ATT)


def short_conv(u, w, b):
    up = jnp.pad(u, ((0, 0), (1, 1), (0, 0)))
    return w[0] * up[:, :-2] + w[1] * up[:, 1:-1] + w[2] * up[:, 2:] + b


def implicit_filter(L, f_w1, f_b1, f_w2, f_b2, f_w3, f_freq):
    f32 = jnp.float32
    tpos = jnp.arange(L, dtype=f32)
    t_norm = tpos / max(L - 1, 1)
    w = 2.0 * math.pi * tpos / L
    bands = jnp.linspace(1e-4, POS_BANDS - 1, POS_BANDS, dtype=f32)
    z = jnp.concatenate([t_norm[:, None], jnp.cos(w[:, None] * bands), -jnp.sin(w[:, None] * bands)], axis=-1)
    freq = f_freq.astype(f32)
    hdn = jnp.sin(freq * (z @ f_w1.astype(f32) + f_b1.astype(f32)))
    hdn = jnp.sin(freq * (hdn @ f_w2.astype(f32) + f_b2.astype(f32)))
    h = hdn @ f_w3.astype(f32)
    max_decay = math.log(DECAY_TARGET) / FAST_DECAY_PCT
    min_decay = math.log(DECAY_TARGET) / SLOW_DECAY_PCT
    deltas = jnp.abs(jnp.linspace(min_decay, max_decay, D_HY, dtype=f32))
    deltas = jnp.concatenate([deltas, deltas])
    h = h * (jnp.exp(-t_norm[:, None] * deltas) + DECAY_SHIFT)
    h_f, h_b = h[:, :D_HY], h[:, D_HY:]
    return jnp.concatenate([h_f[:1] + h_b[:1], h_f[1:], jnp.zeros((1, D_HY), f32), h_b[1:][::-1]], axis=0)


def hyena(u_in, conv_w, conv_b, f_w1, f_b1, f_w2, f_b2, f_w3, f_freq, hy_bias):
    L = u_in.shape[1]
    u = short_conv(u_in, conv_w, conv_b)
    x0, x1, vv = jnp.split(u, 3, axis=-1)
    z = vv * x1
    filt = implicit_filter(L, f_w1, f_b1, f_w2, f_b2, f_w3, f_freq)
    zf = jnp.fft.rfft(z.astype(jnp.float32), n=2 * L, axis=1)
    hf = jnp.fft.rfft(filt, axis=0)
    y = jnp.fft.irfft(zf * hf[None], n=2 * L, axis=1)[:, :L]
    y = y.astype(z.dtype) + hy_bias * z
    return x0 * y


def mixer_layer(x, mod, rope, ctx_kv, norm_g, w_in, q_g, k_g, conv_w, conv_b,
                f_w1, f_b1, f_w2, f_b2, f_w3, f_freq, hy_bias, w_out):
    B, L, _ = x.shape
    shift, scale, gate = jnp.split(mod, 3, axis=-1)
    h = rmsnorm(x, norm_g) * (1.0 + scale) + shift
    proj = h @ w_in
    q, k, v, g_att, hy_in, g_hy = jnp.split(proj, [SPLIT_Q, SPLIT_K, SPLIT_V, SPLIT_GA, SPLIT_HY], axis=-1)
    q = rmsnorm(q.reshape(B, L, N_HEADS, HEAD_DIM), q_g)
    k = rmsnorm(k.reshape(B, L, N_KV_HEADS, HEAD_DIM), k_g)
    v = v.reshape(B, L, N_KV_HEADS, HEAD_DIM)
    if rope is None:
        kv_out = (k, v)
        k_all, v_all = k, v
    else:
        cos, sin = rope
        q = apply_rope(q, cos, sin)
        k = apply_rope(k, cos, sin)
        k_all = jnp.concatenate([k, ctx_kv[0]], axis=1)
        v_all = jnp.concatenate([v, ctx_kv[1]], axis=1)
        kv_out = None
    att = attend(q, k_all, v_all) * jax.nn.silu(g_att)
    hy = hyena(hy_in, conv_w, conv_b, f_w1, f_b1, f_w2, f_b2, f_w3, f_freq, hy_bias) * jax.nn.silu(g_hy)
    out = jnp.concatenate([att, hy], axis=-1) @ w_out
    return x + gate * out, kv_out


def setup_inputs(seed: int = 0) -> dict:
    key = jax.random.key(seed)
    ks = jax.random.split(key, 26)

    def nrm(k, shape, s):
        return jax.random.normal(k, shape, jnp.float32) * s

    return {
        "x_prompt": nrm(ks[0], (BATCH, SEQ, D_MODEL), 1.0),
        "x_sample": nrm(ks[1], (DEC_BATCH, DEC_SEQ, D_MODEL), 1.0),
        "cache_k": nrm(ks[2], (DEC_BATCH, DEPTH, PAST_LEN, N_KV_HEADS, HEAD_DIM), 1.0),
        "cache_v": nrm(ks[3], (DEC_BATCH, DEPTH, PAST_LEN, N_KV_HEADS, HEAD_DIM), 1.0),
        "c": nrm(ks[4], (DEC_BATCH, D_MODEL), 1.0),
        "c_ctx": nrm(ks[5], (D_MODEL,), 1.0),
        "norm_g": 1.0 + nrm(ks[6], (DEPTH, D_MODEL), 0.1),
        "w_ada": nrm(ks[7], (DEPTH, D_MODEL, 3 * D_MODEL), 0.02),
        "b_ada": nrm(ks[8], (DEPTH, 3 * D_MODEL), 0.01),
        "w_in": nrm(ks[9], (DEPTH, D_MODEL, D_IN), D_MODEL ** -0.5),
        "q_norm_g": 1.0 + nrm(ks[10], (DEPTH, HEAD_DIM), 0.1),
        "k_norm_g": 1.0 + nrm(ks[11], (DEPTH, HEAD_DIM), 0.1),
        "conv_w": nrm(ks[12], (DEPTH, SHORT_CONV, 3 * D_HY), 0.5),
        "conv_b": nrm(ks[13], (DEPTH, 3 * D_HY), 0.02),
        "filt_w1": nrm(ks[14], (DEPTH, POS_EMB, FILT_HID), POS_EMB ** -0.5),
        "filt_b1": nrm(ks[15], (DEPTH, FILT_HID), 0.02),
        "filt_w2": nrm(ks[16], (DEPTH, FILT_HID, FILT_HID), FILT_HID ** -0.5),
        "filt_b2": nrm(ks[17], (DEPTH, FILT_HID), 0.02),
        "filt_w3": nrm(ks[18], (DEPTH, FILT_HID, 2 * D_HY), 0.02),
        "filt_freq": 1.0 + nrm(ks[19], (DEPTH, FILT_HID), 0.1),
        "hy_bias": nrm(ks[20], (DEPTH, D_HY), 0.1),
        "w_out": nrm(ks[21], (DEPTH, D_MIX, D_MODEL), D_MIX ** -0.5),
        "final_norm_g": 1.0 + nrm(ks[22], (D_MODEL,), 0.1),
    }


def reference(x_prompt, x_sample, cache_k, cache_v, c, c_ctx, norm_g, w_ada, b_ada, w_in,
              q_norm_g, k_norm_g, conv_w, conv_b, filt_w1, filt_b1, filt_w2, filt_b2,
              filt_w3, filt_freq, hy_bias, w_out, final_norm_g):
    rope = rope_tables(x_sample.shape[1])
    ctx = x_prompt
    lat = x_sample
    ks_out = []
    vs_out = []
    for l in range(DEPTH):
        lw = (norm_g[l], w_in[l], q_norm_g[l], k_norm_g[l], conv_w[l], conv_b[l],
              filt_w1[l], filt_b1[l], filt_w2[l], filt_b2[l], filt_w3[l], filt_freq[l],
              hy_bias[l], w_out[l])
        mod_ctx = (jax.nn.silu(c_ctx) @ w_ada[l] + b_ada[l])[None, None, :]
        mod_lat = (jax.nn.silu(c) @ w_ada[l] + b_ada[l])[:, None, :]
        ctx, kv = mixer_layer(ctx, mod_ctx, None, None, *lw)
        ks_out.append(kv[0])
        vs_out.append(kv[1])
        lat, _ = mixer_layer(lat, mod_lat, rope, (cache_k[:, l], cache_v[:, l]), *lw)
    y_prompt = rmsnorm(ctx, final_norm_g)
    y_sample = rmsnorm(lat, final_norm_g)
    new_k = jnp.stack(ks_out, axis=1)
    new_v = jnp.stack(vs_out, axis=1)
    return (y_prompt, y_sample, new_k, new_v)
```

```python
import math
import numpy as np
import ml_dtypes
import concourse.bass as bass
import concourse.mybir as mybir
from concourse.bass_utils import run_bass_kernel_spmd

F32 = mybir.dt.float32
BF16 = mybir.dt.bfloat16
AF = mybir.ActivationFunctionType
ALU = mybir.AluOpType
AX = mybir.AxisListType
NPBF = ml_dtypes.bfloat16

D = 2048
DEPTH = 2
DIN = 7168
EPS = 1e-6
PI = math.pi
MAGIC = 12582912.0


class Tok:
    __slots__ = ("sem", "val", "eng", "grp")

    def __init__(self, sem, val, eng, grp=None):
        self.sem, self.val, self.eng, self.grp = sem, val, eng, grp


class T:
    def __init__(self, name=""):
        self.name = name
        self.w = None
        self.r = {}


class S:
    def __init__(self, nc):
        self.nc = nc
        self.E = {"pe": nc.tensor, "dve": nc.vector, "act": nc.scalar, "pool": nc.gpsimd, "sp": nc.sync}
        self.sem = {e: nc.alloc_semaphore(name="s_" + e) for e in self.E}
        self.cnt = {e: 0 for e in self.E}
        self.known = {e: {} for e in self.E}
        self.dsem = {}
        self.dcnt = {}
        self.ninst = 0
        self.stop = None
        self.marks = 0
        self.dead = False

    def dgroup(self, name):
        self.dsem[name] = self.nc.alloc_semaphore(name="d_" + name)
        self.dcnt[name] = 0
        return name

    def _wait(self, e, tok):
        if tok is None:
            return
        val = tok.val
        if tok.eng == "dma":
            val = 16 * self.dcnt[tok.grp]
        elif tok.eng == e and e == "pe":
            return
        key = id(tok.sem)
        if self.known[e].get(key, 0) >= val:
            return
        self.E[e].wait_ge(tok.sem, val)
        self.known[e][key] = val

    def deps(self, e, reads, writes):
        for t in reads:
            self._wait(e, t.w)
        for t in writes:
            self._wait(e, t.w)
            for r in t.r.values():
                self._wait(e, r)

    def done(self, tok, reads, writes):
        key = tok.grp if tok.eng == "dma" else tok.eng
        for t in reads:
            t.r[key] = tok
        for t in writes:
            t.w = tok
            t.r = {}

    def op(self, e, fn, reads, writes):
        if self.dead:
            return None
        self.deps(e, reads, writes)
        ins = fn()
        self.cnt[e] += 1
        ins.then_inc(self.sem[e], 1)
        tok = Tok(self.sem[e], self.cnt[e], e)
        self.done(tok, reads, writes)
        self.ninst += 1
        return tok

    def pe(self, fns, reads, writes):
        if self.dead:
            return None
        self.deps("pe", reads, writes)
        ins = None
        for fn in fns:
            ins = fn()
            self.ninst += 1
        self.cnt["pe"] += 1
        ins.then_inc(self.sem["pe"], 1)
        tok = Tok(self.sem["pe"], self.cnt["pe"], "pe")
        self.done(tok, reads, writes)
        return tok

    def dma(self, e, grp, out, in_, reads, writes):
        if self.dead:
            return None
        self.deps(e, reads, writes)
        ins = self.E[e].dma_start(out=out, in_=in_)
        self.dcnt[grp] += 1
        ins.then_inc(self.dsem[grp], 16)
        tok = Tok(self.dsem[grp], 16 * self.dcnt[grp], "dma", grp)
        self.done(tok, reads, writes)
        self.ninst += 1
        return tok

    def cut(self, n):
        import os
        c = os.environ.get("KCUT")
        if c is None or self.dead:
            return
        cn, ck = (c.split(":") + ["1"])[:2]
        if int(cn) != n:
            return
        self.cutcnt = getattr(self, "cutcnt", 0) + 1
        if self.cutcnt >= int(ck):
            print("CUT at", n)
            self.dead = True
            self.dead = False
            st, self.stop = self.stop, None
            self.barrier()
            self.stop = st
            self.dead = True

    def barrier(self):
        self.marks += 1
        if self.dead:
            return
        if self.stop is not None and self.marks >= self.stop:
            self.dead = True
        for e in self.E:
            for o in self.E:
                if o != e and self.cnt[o] > 0:
                    self._wait(e, Tok(self.sem[o], self.cnt[o], o))
            for g in self.dsem:
                if self.dcnt[g] > 0:
                    self._wait(e, Tok(self.dsem[g], 16 * self.dcnt[g], "dma", g))


GROUPS = [
    dict(name="p", nseq=2, L=256, T=512, ctx=0, rope=False, vec=0),
    dict(name="s", nseq=1, L=1024, T=1024, ctx=512, rope=True, vec=1),
]


def build_nc(stop=None, fast=False):
    nc = bass.Bass("TRN2", target_bir_lowering=False)

    def din(name, shape, dt=F32):
        return nc.dram_tensor(name, list(shape), dt, kind="ExternalInput").ap()

    def dout(name, shape):
        return nc.dram_tensor(name, list(shape), F32, kind="ExternalOutput").ap()

    def dscr(name, shape):
        return nc.dram_tensor(name, list(shape), F32, kind="Internal").ap()

    xin = {"p": din("xp", [512, D]), "s": din("xs", [1024, D])}
    ck_d = din("ck", [2, 512, 512])
    cv_d = din("cv", [2, 512, 512])
    ccol_d = din("ccol", [128, 32])
    ng_d = din("ng", [128, 32])
    bada_d = din("bada", [128, 96])
    cw_d = din("cw", [128, 144])
    cb_d = din("cb", [128, 48])
    hb_d = din("hb", [128, 16])
    fcol_d = din("fcol", [64, 6])
    gq_d = din("gq", [128, 256])
    gk_d = din("gk", [128, 256])
    fng_d = din("fng", [128, D])
    wada_d = din("wada", [2, 128 if fast else D, 3 * D])
    win_d = din("win", [1 if fast else 2, D, DIN])
    wout_d = din("wout", [1 if fast else 2, D, D])
    fw1_d = din("fw1", [2, 33, 64])
    fw2_d = din("fw2", [2, 64, 64])
    fw3_d = din("fw3", [2, 64, 2048])
    identb_d = din("identb", [128, 128], BF16)
    identf_d = din("identf", [128, 128])
    Bm_d = {256: din("B256", [128, 2, 512], BF16), 1024: din("B1024", [128, 8, 2048], BF16)}
    Gm_d = {256: din("G256", [128, 4, 256], BF16), 1024: din("G1024", [128, 16, 1024], BF16)}
    zp_d = {256: din("zp256", [33, 256]), 1024: din("zp1024", [33, 1024])}
    wn_d = {256: din("wn256", [128, 2, 1024]), 1024: din("wn1024", [128, 8, 1024])}
    rC_d = din("ropeC", [128, 8, 128])
    rS_d = din("ropeS", [128, 8, 64])
    rN_d = din("ropeN", [128, 8, 64])

    yout = {"p": dout("yp", [512, D]), "s": dout("ys", [1024, D])}
    nk_d = dout("nk", [2, 2, 256, 512])
    nv_d = dout("nv", [2, 2, 256, 512])
    xsc = {"p": dscr("xscp", [512, D]), "s": dscr("xscs", [1024, D])}
    Hs_d = {}
    for g in GROUPS:
        for l in range(DEPTH):
            Hs_d[(g["name"], l)] = dscr("Hs_%s%d" % (g["name"], l), [128, 2 * g["L"] // 128, 1024])

    def sb(name, shape, dt=F32):
        return nc.alloc_sbuf_tensor("sb_" + name, list(shape), dt)

    mixT = sb("mixT", [128, 16, 1024], BF16)
    zfm = sb("zfm", [128, 8, 1024], BF16)
    R1 = sb("R1", [128, 16384], BF16)
    R2 = sb("R2", [128, 20480], BF16)
    WB = sb("WB", [128, 16384], BF16)
    SCR = sb("SCR", [128, 16384], BF16)
    identb = sb("identb", [128, 128], BF16)
    identf = sb("identf", [128, 128])
    onesb = sb("onesb", [128, 128], BF16)
    onesf = sb("onesf", [128, 128])
    ccol = sb("ccol", [128, 32])
    scb = sb("scb", [128, 16, 2], BF16)
    modc = sb("modc", [128, 2, 48, 2])
    gsc = sb("gsc", [128, 2, 2, 16])
    ngc = sb("ngc", [128, 2, 16])
    badac = sb("badac", [128, 2, 48])
    cwc = sb("cwc", [128, 2, 72])
    cbc = sb("cbc", [128, 2, 24])
    hbc = sb("hbc", [128, 2, 8])
    fcol = sb("fcol", [64, 2, 3])
    ffb = sb("ffb", [64, 2, 2])
    gqb = sb("gqb", [128, 2, 128])
    gkb = sb("gkb", [128, 2, 128])
    WB2 = sb("WB2", [128, 8192], BF16)
    small = sb("small", [128, 64])
    Dg = sb("Dg", [128, 128])

    def view(arena, off, shape, dt):
        nb = int(np.prod(shape)) * (4 if dt == F32 else 2)
        v = arena[:, off // 2:(off + nb) // 2]
        if dt == F32:
            v = v.bitcast(F32)
        if len(shape) == 1:
            return v
        names = " ".join("a%d" % i for i in range(len(shape)))
        kw = {"a%d" % i: shape[i] for i in range(len(shape))}
        return v.rearrange("p (%s) -> p %s" % (names, names), **kw)

    PA = nc.alloc_psum_tensor("PA", [128, 1024], F32)
    PB = nc.alloc_psum_tensor("PB", [128, 1024], F32)
    PC = nc.alloc_psum_tensor("PC", [128, 1024], F32)
    PT = [nc.alloc_psum_tensor("PT0", [128, 1024], BF16), nc.alloc_psum_tensor("PT1", [128, 1024], BF16)]
    TPT = [T("pt0"), T("pt1")]
    banks = []
    for nm, P in (("a", PA), ("b", PB), ("c", PC)):
        for h in range(2):
            banks.append((P[:, h * 512:(h + 1) * 512], T(nm + str(h))))
    pairs = [(PA, banks[0][1], banks[1][1]), (PB, banks[2][1], banks[3][1]), (PC, banks[4][1], banks[5][1])]

    s = S(nc)
    s.stop = stop
    gC = s.dgroup("c")
    gW = s.dgroup("w")
    gX = s.dgroup("x")
    gO = s.dgroup("o")
    gM = s.dgroup("m")
    gB = [s.dgroup("b%d" % i) for i in range(8)]
    gK = s.dgroup("k")

    rr = {"bank": 0, "pair": 0, "pt": 0, "slot": 0}

    def nbank():
        rr["bank"] = (rr["bank"] + 1) % 6
        return banks[rr["bank"]]

    def npair():
        rr["pair"] = (rr["pair"] + 1) % 3
        return pairs[rr["pair"]]

    def npt():
        rr["pt"] = (rr["pt"] + 1) % 2
        return PT[rr["pt"]], TPT[rr["pt"]]

    V, SC, AC, PL = nc.vector, nc.scalar, nc.tensor, nc.gpsimd

    TC = T("consts")
    cl = [
        (identb[:], identb_d), (identf[:], identf_d), (ccol[:], ccol_d),
        (ngc[:], ng_d.rearrange("p (l k) -> p l k", l=2)),
        (badac[:], bada_d.rearrange("p (l k) -> p l k", l=2)),
        (cwc[:], cw_d.rearrange("p (l k) -> p l k", l=2)),
        (cbc[:], cb_d.rearrange("p (l k) -> p l k", l=2)),
        (hbc[:], hb_d.rearrange("p (l k) -> p l k", l=2)),
        (fcol[:], fcol_d.rearrange("p (l k) -> p l k", l=2)),
        (gqb[:], gq_d.rearrange("p (l k) -> p l k", l=2)),
        (gkb[:], gk_d.rearrange("p (l k) -> p l k", l=2)),
    ]
    for o, i in cl:
        s.dma("sp", gC, o, i, [], [TC])
    s.op("dve", lambda: V.memset(onesb[:], 1.0), [], [TC])
    s.op("dve", lambda: V.memset(onesf[:], 1.0), [TC], [TC])
    for l in range(2):
        for j in range(2):
            s.op("dve", lambda l=l, j=j: V.tensor_tensor(out=ffb[:, l, j:j + 1], in0=fcol[:, l, j:j + 1],
                                                         in1=fcol[:, l, 2:3], op=ALU.mult), [TC], [TC])
    s.op("act", lambda: SC.activation(out=scb[:].rearrange("p k v -> p v k"),
                                      in_=ccol[:].rearrange("p (v k) -> p v k", v=2), func=AF.Silu), [TC], [TC])
    s.barrier()

    wslots = [view(WB, 0, [16, 512], BF16), view(WB, 16384, [16, 512], BF16), view(WB2, 0, [16, 512], BF16)]
    Tslots = [T("w0"), T("w1"), T("w2")]
    gWs = [s.dgroup("w0"), s.dgroup("w1"), s.dgroup("w2")]

    def load_w(src):
        rr["slot"] = (rr["slot"] + 1) % 3
        sl, ts = wslots[rr["slot"]], Tslots[rr["slot"]]
        s.dma("pool", gWs[rr["slot"]], sl, src.rearrange("(k p) c -> p k c", p=128), [], [ts])
        return sl, ts

    def stream(blocks):
        q = []
        nx = 0
        while nx < min(2, len(blocks)):
            q.append(load_w(blocks[nx]))
            nx += 1
        for i in range(len(blocks)):
            cur = q.pop(0)
            if nx < len(blocks):
                q.append(load_w(blocks[nx]))
                nx += 1
            yield i, cur[0], cur[1]

    class WStream:
        def __init__(self, blocks):
            self.blocks, self.q, self.nx = blocks, [], 0

        def prime(self, depth=2):
            while len(self.q) < depth and self.nx < len(self.blocks):
                self.q.append(load_w(self.blocks[self.nx]))
                self.nx += 1

        def take(self, n):
            for i in range(n):
                self.prime(1)
                cur = self.q.pop(0)
                self.prime(2)
                yield i, cur[0], cur[1]

    FT = {"wn": [T(), T()], "hst": [T(), T()], "f": T(), "t": T(), "h": T(), "B": T(),
          "w3": T(), "w3b": T(), "zp": T(), "w1": T(), "w2": T()}

    def filter_phase(g, l):
        L = g["L"]
        NT = L // 128
        NR = 2 * NT
        Bm = view(R1, 0, [NT, 2 * L], BF16)
        hp = view(R2, 0, [NT, 1024], BF16)
        hm = view(R2, NT * 2048, [NT, 1024], BF16)
        MX = mixT[:].rearrange("p a b -> p (a b)")
        fw3 = view(MX, 0, [2048], F32)
        zp = view(MX, 8192, [L], F32)
        h1 = view(MX, 12288, [L], F32)
        ZF = zfm[:].rearrange("p a b -> p (a b)")
        h2 = view(ZF, 4096, [L], BF16)
        fw3b = view(ZF, 0, [2048], BF16)
        aa = view(MX, 20480, [L], F32)
        kt = view(MX, 25600, [L], F32)
        fw1 = view(MX, 24576, [64], F32)
        fw2 = view(MX, 24576 + 256, [64], F32)
        wnt = [view(SCR, 0, [1024], F32), view(SCR, 4096, [1024], F32)]
        Hst = [view(SCR, 8192, [1024], F32), view(SCR, 12288, [1024], F32)]
        t1 = view(SCR, 16384, [512], F32)
        t2 = view(SCR, 18432, [512], F32)
        Twn, THst, Tf, Tt, Th, TB = FT["wn"], FT["hst"], FT["f"], FT["t"], FT["h"], FT["B"]
        if FT.get("BL") != L:
            s.dma("sp", gM, Bm, Bm_d[L], [], [TB])
            FT["BL"] = L
        Tw3, Tw3b, Tzp, Tw1, Tw2 = FT["w3"], FT["w3b"], FT["zp"], FT["w1"], FT["w2"]
        s.dma("sp", gB[2], fw3[0:64, :], fw3_d[l], [], [Tw3])
        s.dma("sp", gB[3], zp[0:33, :], zp_d[L], [], [Tzp])
        s.dma("sp", gB[6], fw1[0:33, :], fw1_d[l], [], [Tw1])
        s.dma("sp", gB[7], fw2[0:64, :], fw2_d[l], [], [Tw2])
        s.op("dve", lambda: V.tensor_copy(out=fw3b[0:64, :], in_=fw3[0:64, :]), [Tw3], [Tw3b])
        n = min(L, 512)
        for stage in range(2):
            src = zp if stage == 0 else h1
            dst = h1 if stage == 0 else h2
            wt = fw1 if stage == 0 else fw2
            kk = 33 if stage == 0 else 64
            for sp in range(L // n):
                ps, tp = nbank()
                s.pe([lambda ps=ps, wt=wt, src=src, sp=sp, kk=kk: AC.matmul(ps[0:64, 0:n], wt[0:kk, :],
                                                                            src[0:kk, sp * n:(sp + 1) * n],
                                                                            start=True, stop=True)],
                     [Tzp, Tw1] if stage == 0 else [Tw2, Tf], [tp])
                s.op("dve", lambda ps=ps, sp=sp, stage=stage: V.tensor_scalar(
                    out=aa[0:64, sp * n:(sp + 1) * n], in0=ps[0:64, 0:n], scalar1=fcol[:, l, 2:3],
                    scalar2=ffb[:, l, stage:stage + 1], op0=ALU.mult, op1=ALU.add), [tp, TC], [Tf])
                asl = aa[0:64, sp * n:(sp + 1) * n]
                ksl = kt[0:64, sp * n:(sp + 1) * n]
                s.op("dve", lambda asl=asl, ksl=ksl: V.tensor_scalar(
                    out=ksl, in0=asl, scalar1=1.0 / (2.0 * PI), scalar2=MAGIC, op0=ALU.mult, op1=ALU.add),
                     [Tf], [Tf])
                s.op("dve", lambda ksl=ksl: V.tensor_scalar(
                    out=ksl, in0=ksl, scalar1=MAGIC, scalar2=2.0 * PI, op0=ALU.subtract, op1=ALU.mult),
                     [Tf], [Tf])
                s.op("dve", lambda asl=asl, ksl=ksl: V.tensor_tensor(out=asl, in0=asl, in1=ksl, op=ALU.subtract),
                     [Tf], [Tf])
                s.op("dve", lambda asl=asl: V.tensor_scalar(
                    out=asl, in0=asl, scalar1=-PI, scalar2=PI, op0=ALU.max, op1=ALU.min), [Tf], [Tf])
                s.op("act", lambda sp=sp, dst=dst: SC.activation(out=dst[0:64, sp * n:(sp + 1) * n],
                                                                 in_=aa[0:64, sp * n:(sp + 1) * n], func=AF.Sin),
                     [Tf], [Tf])
                yield
        for tt in range(NT):
            wb, twb = wnt[tt % 2], Twn[tt % 2]
            s.dma("sp", gB[tt % 2], wb, wn_d[L][:, tt, :], [], [twb])
            for cf in range(2):
                psf, tpf = nbank()
                psb, tpb = nbank()
                s.pe([lambda psf=psf, tt=tt, cf=cf: AC.matmul(psf, h2[0:64, tt * 128:(tt + 1) * 128],
                                                              fw3b[0:64, cf * 512:(cf + 1) * 512], start=True,
                                                              stop=True)], [Tf, Tw3b], [tpf])
                s.pe([lambda psb=psb, tt=tt, cf=cf: AC.matmul(psb, h2[0:64, tt * 128:(tt + 1) * 128],
                                                              fw3b[0:64, 1024 + cf * 512:1024 + (cf + 1) * 512],
                                                              start=True, stop=True)], [Tf, Tw3b], [tpb])
                wsl = wb[:, cf * 512:(cf + 1) * 512]
                s.op("dve", lambda psf=psf, wsl=wsl: V.tensor_tensor(out=t1, in0=psf, in1=wsl, op=ALU.mult),
                     [tpf, twb], [Tt])
                s.op("dve", lambda psb=psb, wsl=wsl: V.tensor_tensor(out=t2, in0=psb, in1=wsl, op=ALU.mult),
                     [tpb, twb, Tt], [Tt])
                s.op("dve", lambda tt=tt, cf=cf: V.tensor_tensor(out=hp[:, tt, cf * 512:(cf + 1) * 512], in0=t1,
                                                                 in1=t2, op=ALU.add), [Tt], [Th])
                s.op("dve", lambda tt=tt, cf=cf: V.tensor_tensor(out=hm[:, tt, cf * 512:(cf + 1) * 512], in0=t1,
                                                                 in1=t2, op=ALU.subtract), [Tt], [Th, Tt])
            yield
        for rc in range(NR):
            hsrc = hp if rc < NT else hm
            hb_, thb = Hst[rc % 2], THst[rc % 2]
            for half in range(2):
                ps, tp = nbank()
                s.pe([lambda ps=ps, tc=tc, rc=rc, half=half, hsrc=hsrc: AC.matmul(
                    ps, Bm[:, tc, rc * 128:(rc + 1) * 128], hsrc[:, tc, half * 512:(half + 1) * 512],
                    start=(tc == 0), stop=(tc == NT - 1)) for tc in range(NT)], [TB, Th], [tp])
                s.op("act", lambda ps=ps, half=half, hb_=hb_: SC.copy(out=hb_[:, half * 512:(half + 1) * 512],
                                                                      in_=ps), [tp], [thb])
            s.dma("sp", gB[4 + rc % 2], Hs_d[(g["name"], l)][:, rc, :], hb_, [thb], [])
            yield

    def all_filters():
        for g in GROUPS[::-1]:
            for l in range(DEPTH):
                if not fast:
                    yield from filter_phase(g, l)

    Tmod = T("mod")
    filt_holder = [all_filters()]
    modr = [view(SCR, 20480, [512], F32), view(SCR, 22528, [512], F32)]
    Tmr = [T(), T()]

    def pull(nsteps):
        if not filt_holder:
            return
        for _ in range(nsteps):
            try:
                next(filt_holder[0])
            except StopIteration:
                return

    if fast:
        s.op("dve", lambda: V.memset(modc[:], 0.5), [], [Tmod])
    for l in range(0 if fast else DEPTH):
        blocks = [wada_d[l][:, b * 512:(b + 1) * 512] for b in range(12)]
        for b, sl, ts in stream(blocks):
            pull(3)
            ps, tp = nbank()
            s.pe([lambda k=k, sl=sl, ps=ps: AC.matmul(ps[0:2, :], scb[:, k, :], sl[:, k, :], start=(k == 0),
                                                      stop=(k == 15)) for k in range(16)], [ts, TC], [tp])
            mr, tmr = modr[b % 2], Tmr[b % 2]
            s.op("act", lambda ps=ps, mr=mr: SC.copy(out=mr[0:2, :], in_=ps[0:2, :]), [tp], [tmr])
            ps2, tp2 = nbank()
            s.pe([lambda jj=jj, mr=mr, ps2=ps2: AC.matmul(ps2[:, jj * 2:(jj + 1) * 2],
                                                          mr[0:2, jj * 128:(jj + 1) * 128], identf[0:2, 0:2],
                                                          start=True, stop=True) for jj in range(4)],
                 [tmr, TC], [tp2])
            bsl = badac[:, l, b * 4:(b + 1) * 4].rearrange("p (j o) -> p j o", o=1).to_broadcast([128, 4, 2])
            s.op("dve", lambda l=l, b=b, ps2=ps2, bsl=bsl: V.tensor_tensor(
                out=modc[:, l, b * 4:(b + 1) * 4, :], in0=ps2[:, 0:8].rearrange("p (j v) -> p j v", v=2),
                in1=bsl, op=ALU.add), [tp2, TC], [Tmod])
    for l in range(DEPTH):
        for v in range(2):
            s.op("dve", lambda l=l, v=v: V.tensor_scalar(out=gsc[:, l, v, :], in0=modc[:, l, 16:32, v], scalar1=1.0,
                                                         scalar2=None, op0=ALU.add), [Tmod], [Tmod])
            s.op("dve", lambda l=l, v=v: V.tensor_tensor(out=gsc[:, l, v, :], in0=gsc[:, l, v, :], in1=ngc[:, l, :],
                                                         op=ALU.mult), [Tmod, TC], [Tmod])
    pull(100000)
    s.barrier()

    def layer(g, l):
        name, L, T_, nseq, ctx, vec = g["name"], g["L"], g["T"], g["nseq"], g["ctx"], g["vec"]
        NTI = T_ // 128
        NT = L // 128
        NR = 2 * NT
        xsrc = xin[name] if l == 0 else xsc[name]
        hT = view(R1, 0, [16, 1024], BF16)
        qT = view(R2, 0, [8, 1024], BF16)
        kT = view(R2, 16384, [4, 1536], BF16)
        Vt = view(R2, 16384 + 12288, [12, 512], BF16)
        TqT, TkT, TV, Tmix, Tz = T("qT"), T("kT"), T("V"), T("mix"), T("z")
        ThTk = [T("hT%d" % k_) for k_ in range(16)]

        kinds = [("ga", 2048), ("ga", 2560), ("gh", 6144), ("gh", 6656), ("x0", 3072), ("x0", 3584),
                 ("x1", 4096), ("x1", 4608), ("vv", 5120), ("vv", 5632)]
        ws = WStream([win_d[l][:, b * 512:(b + 1) * 512] for b in range(4)] +
                     [win_d[l][:, c0:c0 + 512] for _, c0 in kinds])
        ws.prime(2)
        xt = [view(SCR, 0, [2048], F32), view(SCR, 8192, [2048], F32)]
        Txt = [T(), T()]
        xnb = [view(SCR, 16384, [2048], BF16), view(SCR, 20480, [2048], BF16)]
        Txn = [T(), T()]
        junk = view(SCR, 24576, [2048], BF16)
        Tjunk = T()
        Tsa = [T(), T(), T()]
        qB, qC = [], []
        pts8 = [(PT[0], TPT[0]), (PT[1], TPT[1])] + [(bk.bitcast(BF16), tb) for bk, tb in banks]
        ptr = [0]

        def npt8():
            ptr[0] = (ptr[0] + 1) % 8
            return pts8[ptr[0]]

        for i in range(NTI):
            xb, txb = xt[i % 2], Txt[i % 2]
            xn, txn = xnb[i % 2], Txn[i % 2]
            ca = 16 + 2 * (i % 3)
            tsm = Tsa[i % 3]
            while qC:
                qC.pop(0)()
            s.dma("sp", gB[i % 2], xb, xsrc[i * 128:(i + 1) * 128, :], [], [txb])
            s.op("act", lambda xb=xb, ca=ca: SC.activation(out=junk, in_=xb, func=AF.Square,
                                                           accum_out=small[:, ca:ca + 1]), [txb], [Tjunk, tsm])
            s.op("dve", lambda ca=ca: V.tensor_scalar(out=small[:, ca + 1:ca + 2], in0=small[:, ca:ca + 1],
                                                      scalar1=1.0 / D, scalar2=EPS, op0=ALU.mult, op1=ALU.add),
                 [tsm], [tsm])
            s.op("act", lambda ca=ca: SC.activation(out=small[:, ca + 1:ca + 2], in_=small[:, ca + 1:ca + 2],
                                                    func=AF.Sqrt), [tsm], [tsm])

            def stageB(xb=xb, txb=txb, xn=xn, txn=txn, ca=ca, tsm=tsm, i=i):
                s.op("dve", lambda: V.reciprocal(out=small[:, ca + 1:ca + 2], in_=small[:, ca + 1:ca + 2]),
                     [tsm], [tsm])
                s.op("act", lambda: SC.activation(out=xn, in_=xb, func=AF.Identity, scale=small[:, ca + 1:ca + 2]),
                     [txb, tsm], [txn])

                def stageC():
                    for q in range(4):
                        pt, tpt = npt8()
                        s.pe([lambda pt=pt, q=q, kk=kk: AC.transpose(
                            pt[:, kk * 128:(kk + 1) * 128], xn[:, (q * 4 + kk) * 128:(q * 4 + kk + 1) * 128],
                            identb[:]) for kk in range(4)], [txn, TC], [tpt])
                        for kk in range(4):
                            k = q * 4 + kk
                            if q != 3:
                                s.op("dve", lambda pt=pt, k=k, kk=kk: V.tensor_scalar(
                                    out=hT[:, k, i * 128:(i + 1) * 128], in0=pt[:, kk * 128:(kk + 1) * 128],
                                    scalar1=gsc[:, l, vec, k:k + 1], scalar2=modc[:, l, k, vec:vec + 1],
                                    op0=ALU.mult, op1=ALU.add), [tpt, Tmod], [ThTk[k]])
                            else:
                                s.op("act", lambda pt=pt, k=k, kk=kk: SC.activation(
                                    out=hT[:, k, i * 128:(i + 1) * 128], in_=pt[:, kk * 128:(kk + 1) * 128],
                                    func=AF.Identity, scale=gsc[:, l, vec, k:k + 1],
                                    bias=modc[:, l, k, vec:vec + 1]), [tpt, Tmod], [ThTk[k]])

                qC.append(stageC)

            while qB:
                qB.pop(0)()
            qB.append(stageB)
        while qB:
            qB.pop(0)()
        while qC:
            qC.pop(0)()
        s.barrier()

        sets = []
        for offs, cols in (((0, 2048, 4096, 6144, 8192, 9216), (4, 8)),
                           ((12288, 14336, 24576, 26624, 28672, 29696), (8, 12))):
            sets.append(dict(sqt=view(SCR, offs[0], [512], F32), qn=view(SCR, offs[1], [512], F32),
                             rt=view(SCR, offs[2], [512], F32), ut=view(SCR, offs[3], [512], F32),
                             qb=view(SCR, offs[4], [512], BF16), vf=view(SCR, offs[5], [512], F32),
                             sm=small[:, cols[0]:cols[1]], c0=cols[0],
                             Tsq=T(), Tqn=T(), Trt=T(), Tut=T(), Tqb=T(), Tvf=T(), Tsm=T()))
        ckb = view(zfm[:].rearrange("p a b -> p (a b)"), 0, [4, 512], BF16)
        Tck = T()
        ropeC = view(SCR, 16384, [8, 128], F32)
        ropeS = view(SCR, 20480, [8, 64], F32)
        ropeN = view(SCR, 22528, [8, 64], F32)
        Trope = T()
        if g["rope"]:
            s.dma("sp", gM, ropeC, rC_d, [], [Trope])
            s.dma("sp", gM, ropeS, rS_d, [], [Trope])
            s.dma("sp", gM, ropeN, rN_d, [], [Trope])
        if ctx:
            s.dma("pool", gK, ckb, ck_d[l].rearrange("(j p) c -> p j c", p=128), [], [Tck])
            s.dma("pool", gK, Vt[:, 8:12, :], cv_d[l].rearrange("(j p) c -> p j c", p=128), [], [TV])
        qB, qC = [], []
        Tsm3 = [T(), T(), T()]
        for b, sl, ts in ws.take(4):
            for i in range(NTI):
                bs = sets[(b * NTI + i) % 2]
                ps, tp = nbank()
                s.pe([lambda ps=ps, k=k, i=i, sl=sl: AC.matmul(ps, hT[:, k, i * 128:(i + 1) * 128], sl[:, k, :],
                                                               start=(k == 0), stop=(k == 15)) for k in range(16)],
                     ThTk + [ts], [tp])
                if b == 3:
                    vf, Tvf = bs["vf"], bs["Tvf"]
                    if not ctx:
                        s.op("act", lambda ps=ps, vf=vf: SC.copy(out=vf, in_=ps), [tp], [Tvf])
                        s.dma("sp", gB[6 + (b * NTI + i) % 2], nv_d[i // 2, l, (i % 2) * 128:(i % 2 + 1) * 128, :],
                              vf, [Tvf], [])
                        s.op("dve", lambda i=i, vf=vf: V.tensor_copy(out=Vt[:, i, :], in_=vf), [Tvf], [TV])
                    else:
                        s.op("act", lambda ps=ps, i=i: SC.copy(out=Vt[:, i, :], in_=ps), [tp], [TV])
                    while qC:
                        qC.pop(0)()
                    while qB:
                        qB.pop(0)()
                    continue
                sqt, Tsq = bs["sqt"], bs["Tsq"]
                c0 = 4 + 4 * ((b * NTI + i) % 3)
                smv, Tsm = small[:, c0:c0 + 4], Tsm3[(b * NTI + i) % 3]
                for h in range(4):
                    s.op("act", lambda ps=ps, h=h, sqt=sqt, c0=c0: SC.activation(
                        out=sqt[:, h * 128:(h + 1) * 128], in_=ps[:, h * 128:(h + 1) * 128], func=AF.Square,
                        accum_out=small[:, c0 + h:c0 + h + 1]), [tp], [Tsq, Tsm])
                s.op("dve", lambda smv=smv: V.tensor_scalar(out=smv, in0=smv, scalar1=1.0 / 128, scalar2=EPS,
                                                            op0=ALU.mult, op1=ALU.add), [Tsm], [Tsm])
                s.op("act", lambda smv=smv: SC.activation(out=smv, in_=smv, func=AF.Sqrt), [Tsm], [Tsm])

                def stageB(bs=bs, ps=ps, tp=tp, b=b, i=i, smv=smv, c0=c0, Tsm=Tsm):
                    qn, rt, ut, qb = bs["qn"], bs["rt"], bs["ut"], bs["qb"]
                    Tqn, Trt, Tut, Tqb = bs["Tqn"], bs["Trt"], bs["Tut"], bs["Tqb"]
                    gb = gqb if b < 2 else gkb
                    s.op("dve", lambda: V.reciprocal(out=smv, in_=smv), [Tsm], [Tsm])
                    for h in range(4):
                        s.op("dve", lambda h=h: V.scalar_tensor_tensor(
                            out=qn[:, h * 128:(h + 1) * 128], in0=ps[:, h * 128:(h + 1) * 128],
                            scalar=small[:, c0 + h:c0 + h + 1], in1=gb[:, l, :], op0=ALU.mult, op1=ALU.mult),
                             [tp, Tsm, TC], [Tqn])
                    if b == 2 and not ctx:
                        s.dma("sp", gB[4 + (b * NTI + i) % 2], nk_d[i // 2, l, (i % 2) * 128:(i % 2 + 1) * 128, :],
                              qn, [Tqn], [])
                    if g["rope"]:
                        q4 = qn.rearrange("p (h i two) -> p h i two", h=4, two=2)
                        u4 = ut.rearrange("p (h i two) -> p h i two", h=4, two=2)
                        Cb = ropeC[:, i:i + 1, :].to_broadcast([128, 4, 128])
                        Sb = ropeS[:, i:i + 1, :].to_broadcast([128, 4, 64])
                        Nb = ropeN[:, i:i + 1, :].to_broadcast([128, 4, 64])
                        s.op("dve", lambda: V.tensor_tensor(out=rt.rearrange("p (h d) -> p h d", h=4),
                                                            in0=qn.rearrange("p (h d) -> p h d", h=4), in1=Cb,
                                                            op=ALU.mult), [Tqn, Trope], [Trt])
                        s.op("dve", lambda: V.tensor_tensor(out=u4[:, :, :, 0], in0=q4[:, :, :, 1], in1=Nb,
                                                            op=ALU.mult), [Tqn, Trope], [Tut])
                        s.op("dve", lambda: V.tensor_tensor(out=u4[:, :, :, 1], in0=q4[:, :, :, 0], in1=Sb,
                                                            op=ALU.mult), [Tqn, Trope], [Tut])
                        s.op("dve", lambda: V.tensor_tensor(out=qb, in0=rt, in1=ut, op=ALU.add), [Trt, Tut], [Tqb])
                    else:
                        s.op("act", lambda: SC.copy(out=qb, in_=qn), [Tqn], [Tqb])
                    if b < 2:
                        dst, td = qT[:, b * 4:(b + 1) * 4, i * 128:(i + 1) * 128], TqT
                    else:
                        dst, td = kT[:, :, i * 128:(i + 1) * 128], TkT

                    def fin():
                        pt, tpt = npt()
                        s.pe([lambda pt=pt, h=h: AC.transpose(pt[:, h * 128:(h + 1) * 128],
                                                              qb[:, h * 128:(h + 1) * 128], identb[:])
                              for h in range(4)], [Tqb, TC], [tpt])
                        s.op("act", lambda pt=pt: SC.copy(
                            out=dst, in_=pt[:, 0:512].rearrange("p (h d) -> p h d", h=4)), [tpt], [td])

                    qC.append(fin)

                while qC:
                    qC.pop(0)()
                while qB:
                    qB.pop(0)()
                qB.append(stageB)
        while qB:
            qB.pop(0)()
        while qC:
            qC.pop(0)()
        if ctx:
            for j in range(4):
                pt, tpt = npt()
                s.pe([lambda pt=pt, j=j, h=h: AC.transpose(pt[:, h * 128:(h + 1) * 128],
                                                           ckb[:, j, h * 128:(h + 1) * 128], identb[:])
                      for h in range(4)], [Tck, TC], [tpt])
                s.op("act", lambda pt=pt, j=j: SC.copy(out=kT[:, :, 1024 + j * 128:1024 + (j + 1) * 128],
                                                       in_=pt[:, 0:512].rearrange("p (h d) -> p h d", h=4)),
                     [tpt], [TkT])
        s.barrier()

        tmp = [view(SCR, 0, [1024], F32), view(SCR, 4096, [1024], F32)]
        Ttmp = [T(), T()]
        nsp = T_ // 512
        cnt = 0
        for b, sl, ts in ws.take(10):
            kind = kinds[b][0]
            for jj in range(4):
                c = (b % 2) * 4 + jj
                P, ta, tb = npair()
                tps = [ta, tb][:nsp]
                for sp in range(nsp):
                    s.pe([lambda P=P, sp=sp, k=k, jj=jj, sl=sl: AC.matmul(
                        P[:, sp * 512:(sp + 1) * 512], sl[:, k, jj * 128:(jj + 1) * 128],
                        hT[:, k, sp * 512:(sp + 1) * 512], start=(k == 0), stop=(k == 15)) for k in range(16)],
                         ThTk + [ts], [tps[sp]])
                if kind in ("ga", "gh"):
                    mc = c if kind == "ga" else 8 + c
                    for sp in range(nsp):
                        s.op("act", lambda P=P, sp=sp, mc=mc: SC.activation(
                            out=mixT[:, mc, sp * 512:(sp + 1) * 512], in_=P[:, sp * 512:(sp + 1) * 512],
                            func=AF.Silu), [tps[sp]], [Tmix])
                    continue
                jc = {"x0": 0, "x1": 8, "vv": 16}[kind] + c
                cnt += 1
                tm, ttm = tmp[cnt % 2], Ttmp[cnt % 2]
                s.op("act", lambda P=P, tm=tm, jc=jc: SC.activation(
                    out=tm[:, 0:T_], in_=P[:, 0:T_], func=AF.Identity, scale=cwc[:, l, 24 + jc:25 + jc],
                    bias=cbc[:, l, jc:jc + 1]), tps + [TC], [ttm])
                for sq in range(nseq):
                    s0 = sq * L
                    s.op("dve", lambda P=P, tm=tm, jc=jc, s0=s0: V.scalar_tensor_tensor(
                        out=tm[:, s0 + 1:s0 + L], in0=P[:, s0:s0 + L - 1], scalar=cwc[:, l, jc:jc + 1],
                        in1=tm[:, s0 + 1:s0 + L], op0=ALU.mult, op1=ALU.add), tps + [TC, ttm], [ttm])
                    s.op("dve", lambda P=P, tm=tm, jc=jc, s0=s0: V.scalar_tensor_tensor(
                        out=tm[:, s0:s0 + L - 1], in0=P[:, s0 + 1:s0 + L], scalar=cwc[:, l, 48 + jc:49 + jc],
                        in1=tm[:, s0:s0 + L - 1], op0=ALU.mult, op1=ALU.add), tps + [TC, ttm], [ttm])
                if kind == "x0":
                    s.op("dve", lambda tm=tm, c=c: V.tensor_tensor(out=mixT[:, 8 + c, 0:T_], in0=tm[:, 0:T_],
                                                                   in1=mixT[:, 8 + c, 0:T_], op=ALU.mult),
                         [ttm, Tmix], [Tmix])
                elif kind == "x1":
                    s.op("act", lambda tm=tm, c=c: SC.copy(out=zfm[:, c, 0:T_], in_=tm[:, 0:T_]), [ttm], [Tz])
                else:
                    s.op("dve", lambda tm=tm, c=c: V.tensor_tensor(out=zfm[:, c, 0:T_], in0=tm[:, 0:T_],
                                                                   in1=zfm[:, c, 0:T_], op=ALU.mult),
                         [ttm, Tz], [Tz])
        s.barrier()

        Gm = view(R1, 0, [NR, L], BF16)
        Hh = view(WB, 0, [NR, 512], F32)
        TGm, THh = T(), T()
        s.dma("sp", gM, Gm, Gm_d[L], [], [TGm])
        wf = WStream([wout_d[l][:, b * 512:(b + 1) * 512] for b in range(4)])
        if L == 256:
            rr["slot"] = 0
            wf.prime(2)
        else:
            rr["slot"] = 1
            wf.prime(1)
        Hfull = view(WB, 0, [NR, 1024], F32) if L == 256 else None
        if L == 256:
            s.dma("sp", gM, Hfull, Hs_d[(name, l)], [], [THh])
        else:
            s.dma("sp", gM, Hh, Hs_d[(name, l)][:, :, 0:512], [], [THh])
        Eb = [view(SCR, j * 1024, [512], BF16) for j in range(4)]
        TE = [T() for _ in range(4)]
        rd = view(SCR, 4096, [512], F32)
        ov = view(SCR, 6144, [512], F32)
        Trd, Tov = T(), T()
        nkc = (L + ctx) // 128
        sm_scale = 1.0 / math.sqrt(128.0)
        ecnt = 0
        acnt = 0
        for sq in range(nseq):
            s0 = sq * L
            for gi in range(4):
                for span in range(L // 256):
                    q0 = s0 + span * 256
                    qv = qT[:, 2 * gi:2 * gi + 2, q0:q0 + 256]
                    acnt += 1
                    psS = [banks[0], banks[1]]
                    psO, tO = banks[2 + 2 * (acnt % 2)]
                    psD, tD = banks[3 + 2 * (acnt % 2)]

                    def smm(kc):
                        ps, tp = psS[kc % 2]
                        if ctx:
                            koff = kc * 128
                        else:
                            koff = s0 + kc * 128
                        s.pe([lambda ps=ps, koff=koff: AC.matmul(ps, kT[:, gi, koff:koff + 128], qv, start=True,
                                                                 stop=True)], [TkT, TqT], [tp])

                    smm(0)
                    for kc in range(nkc):
                        if kc + 1 < nkc:
                            smm(kc + 1)
                        ps, tp = psS[kc % 2]
                        ecnt += 1
                        E, tE = Eb[ecnt % 4], TE[ecnt % 4]
                        s.op("act", lambda ps=ps, E=E: SC.activation(out=E, in_=ps, func=AF.Exp, scale=sm_scale),
                             [tp], [tE])
                        vi = kc if ctx else sq * 2 + kc
                        s.pe([lambda E=E, vi=vi, kc=kc: AC.matmul(psO, Vt[:, vi, gi * 128:(gi + 1) * 128], E,
                                                                  start=(kc == 0), stop=(kc == nkc - 1)),
                              lambda E=E, kc=kc: AC.matmul(psD, onesb[:], E, start=(kc == 0),
                                                           stop=(kc == nkc - 1))], [TV, tE, TC], [tO, tD])
                    s.op("dve", lambda: V.reciprocal(out=rd, in_=psD), [tD], [Trd])
                    s.op("dve", lambda: V.tensor_tensor(out=ov, in0=psO, in1=rd, op=ALU.mult), [tO, Trd], [Tov])
                    mv = mixT[:, 2 * gi:2 * gi + 2, q0:q0 + 256]
                    s.op("dve", lambda mv=mv: V.tensor_tensor(out=mv, in0=ov.rearrange("p (h q) -> p h q", h=2),
                                                              in1=mv, op=ALU.mult), [Tov, Tmix], [Tmix])
        s.barrier()

        Bm = view(R2, 0, [NT, 2 * L], BF16)
        ztm = view(R2, 32768, [NT, 512], BF16)
        Yr = view(SCR, 0, [NR, 512], BF16)
        zc = view(SCR, 16384, [512], F32)
        zs = view(SCR, 18432, [512], F32)
        t1 = view(SCR, 20480, [512], F32)
        t2 = view(SCR, 22528, [512], F32)
        ty = view(SCR, 24576, [1024], F32)
        TBm, Tzt, TYr, Tzc, Tzs, Tt1, Tt2, Tty = (T() for _ in range(8))
        s.dma("sp", gM, Bm, Bm_d[L], [], [TBm])
        n = min(L, 512)
        for sq in range(nseq):
            s0 = sq * L
            for half in range(2):
                if L != 256 and not (sq == 0 and half == 0):
                    s.dma("sp", gM, Hh, Hs_d[(name, l)][:, :, half * 512:(half + 1) * 512], [], [THh])
                for tt in range(NT):
                    pt, tpt = npt()
                    s.pe([lambda pt=pt, cc=cc, tt=tt: AC.transpose(
                        pt[:, cc * 128:(cc + 1) * 128],
                        zfm[:, half * 4 + cc, s0 + tt * 128:s0 + (tt + 1) * 128], identb[:]) for cc in range(4)],
                         [Tz, TC], [tpt])
                    s.op("act", lambda pt=pt, tt=tt: SC.copy(out=ztm[:, tt, :], in_=pt[:, 0:512]), [tpt], [Tzt])
                for j in range(NT):
                    psc, tpc = banks[(j % 2) * 2]
                    pss, tps_ = banks[(j % 2) * 2 + 1]
                    s.pe([lambda psc=psc, tc=tc, j=j: AC.matmul(psc, Bm[:, tc, j * 128:(j + 1) * 128],
                                                                ztm[:, tc, :], start=(tc == 0),
                                                                stop=(tc == NT - 1)) for tc in range(NT)],
                         [TBm, Tzt], [tpc])
                    s.pe([lambda pss=pss, tc=tc, j=j: AC.matmul(pss, Bm[:, tc, (NT + j) * 128:(NT + j + 1) * 128],
                                                                ztm[:, tc, :], start=(tc == 0),
                                                                stop=(tc == NT - 1)) for tc in range(NT)],
                         [TBm, Tzt], [tps_])
                    s.op("act", lambda psc=psc: SC.copy(out=zc, in_=psc), [tpc], [Tzc])
                    s.op("act", lambda pss=pss: SC.copy(out=zs, in_=pss), [tps_], [Tzs])
                    if L == 256:
                        Hc = Hfull[:, j, half * 512:(half + 1) * 512]
                        Hs_ = Hfull[:, NT + j, half * 512:(half + 1) * 512]
                    else:
                        Hc, Hs_ = Hh[:, j, :], Hh[:, NT + j, :]
                    s.op("dve", lambda Hc=Hc: V.tensor_tensor(out=t1, in0=zc, in1=Hc, op=ALU.mult),
                         [Tzc, THh], [Tt1])
                    s.op("dve", lambda Hs_=Hs_: V.tensor_tensor(out=t2, in0=zs, in1=Hs_, op=ALU.mult),
                         [Tzs, THh], [Tt2])
                    s.op("dve", lambda j=j: V.tensor_tensor(out=Yr[:, j, :], in0=t1, in1=t2, op=ALU.subtract),
                         [Tt1, Tt2], [TYr])
                    s.op("dve", lambda Hs_=Hs_: V.tensor_tensor(out=t1, in0=zc, in1=Hs_, op=ALU.mult),
                         [Tzc, THh], [Tt1])
                    s.op("dve", lambda Hc=Hc: V.tensor_tensor(out=t2, in0=zs, in1=Hc, op=ALU.mult),
                         [Tzs, THh], [Tt2])
                    s.op("dve", lambda j=j: V.tensor_tensor(out=Yr[:, NT + j, :], in0=t1, in1=t2, op=ALU.add),
                         [Tt1, Tt2], [TYr])
                for cc in range(4):
                    c = half * 4 + cc
                    P, ta, tb = pairs[2]
                    tps = [ta, tb][:L // n]
                    for sp in range(L // n):
                        s.pe([lambda sp=sp, rc=rc, cc=cc: AC.matmul(
                            P[:, sp * 512:sp * 512 + n], Yr[:, rc, cc * 128:(cc + 1) * 128],
                            Gm[:, rc, sp * n:(sp + 1) * n], start=(rc == 0), stop=(rc == NR - 1))
                              for rc in range(NR)], [TYr, TGm], [tps[sp]])
                    s.op("dve", lambda c=c: V.scalar_tensor_tensor(
                        out=ty[:, 0:L], in0=zfm[:, c, s0:s0 + L], scalar=hbc[:, l, c:c + 1], in1=P[:, 0:L],
                        op0=ALU.mult, op1=ALU.add), tps + [Tz, TC], [Tty])
                    s.op("dve", lambda c=c: V.tensor_tensor(out=mixT[:, 8 + c, s0:s0 + L], in0=ty[:, 0:L],
                                                            in1=mixT[:, 8 + c, s0:s0 + L], op=ALU.mult),
                         [Tty, Tmix], [Tmix])
        s.barrier()

        gbc = view(SCR, 0, [2048], F32)
        xs_ = [view(SCR, 8192, [512], F32), view(SCR, 10240, [512], F32),
               view(SCR, 18432, [512], F32), view(SCR, 20480, [512], F32)]
        xo = [view(SCR, 12288, [512], F32), view(SCR, 14336, [512], F32),
              view(SCR, 22528, [512], F32), view(SCR, 24576, [512], F32)]
        to = view(SCR, 16384, [512], F32)
        Tg, Tto, TDg = T(), T(), T()
        Txs, Txo = [T(), T(), T(), T()], [T(), T(), T(), T()]
        Dgs = [view(SCR, 26624 + 512 * i_, [128], F32) for i_ in range(4)]
        TDgs = [T() for _ in range(4)]
        for k in range(16):
            Dgk, TDgk = Dgs[k % 4], TDgs[k % 4]
            s.op("dve", lambda k=k, Dgk=Dgk: V.tensor_scalar(out=Dgk, in0=identf[:],
                                                             scalar1=modc[:, l, 32 + k, vec:vec + 1],
                                                             scalar2=None, op0=ALU.mult), [TC, Tmod], [TDgk])
            if k % 4 == 0:
                ps, tp = nbank()
            s.pe([lambda ps=ps, k=k, Dgk=Dgk: AC.matmul(ps[:, (k % 4) * 128:(k % 4 + 1) * 128], onesf[:], Dgk,
                                                        start=True, stop=True)], [TDgk, TC], [tp])
            if k % 4 == 3:
                s.op("act", lambda ps=ps, k=k: SC.copy(out=gbc[:, (k // 4) * 512:(k // 4 + 1) * 512], in_=ps),
                     [tp], [Tg])
        blocks = [wout_d[l][:, b * 512:(b + 1) * 512] for b in range(4)]
        xdst = xsc[name]
        TXD = T("xscr")
        it = 0
        order = [(b_, i_) for b_ in range(4) for i_ in range(NTI)]

        def xload(n):
            if n < len(order):
                b_, i_ = order[n]
                s.dma("sp", gB[n % 4], xs_[n % 4], xsrc[i_ * 128:(i_ + 1) * 128, b_ * 512:(b_ + 1) * 512], [],
                      [Txs[n % 4]])

        xload(0)
        xload(1)
        for b, sl, ts in wf.take(4):
            for i in range(NTI):
                xb, txb = xs_[it % 4], Txs[it % 4]
                ob, tob = xo[it % 4], Txo[it % 4]
                gst = gB[4 + it % 4]
                xload(it + 2)
                it += 1
                ps, tp = nbank()
                s.pe([lambda ps=ps, c=c, i=i, sl=sl: AC.matmul(ps, mixT[:, c, i * 128:(i + 1) * 128], sl[:, c, :],
                                                               start=(c == 0), stop=(c == 15)) for c in range(16)],
                     [Tmix, ts], [tp])
                s.op("dve", lambda ps=ps, b=b: V.tensor_tensor(out=to, in0=ps, in1=gbc[:, b * 512:(b + 1) * 512],
                                                               op=ALU.mult), [tp, Tg], [Tto])
                s.op("dve", lambda xb=xb, ob=ob: V.tensor_tensor(out=ob, in0=to, in1=xb, op=ALU.add),
                     [Tto, txb], [tob])
                s.dma("sp", gst, xdst[i * 128:(i + 1) * 128, b * 512:(b + 1) * 512], ob, [tob], [])
        s.barrier()

    def final(g):
        name, T_ = g["name"], g["T"]
        fng = view(SCR, 0, [2048], F32)
        xt = [view(SCR, 8192, [2048], F32), view(SCR, 16384, [2048], F32)]
        yts = [view(SCR, 24576, [2048], F32), view(R1, 8192, [2048], F32)]
        junk = view(R1, 0, [2048], BF16)
        Tf, Tj = T(), T()
        Txt, Tys, Tsms = [T(), T()], [T(), T()], [T(), T()]
        s.dma("sp", gM, fng, fng_d, [], [Tf])
        nti = T_ // 128

        def fload(i):
            if i < nti:
                s.dma("sp", gB[i % 2], xt[i % 2], xsc[name][i * 128:(i + 1) * 128, :], [], [Txt[i % 2]])

        fload(0)
        for i in range(nti):
            xb, txb = xt[i % 2], Txt[i % 2]
            yt, ty = yts[i % 2], Tys[i % 2]
            c0, tsm = 24 + 2 * (i % 2), Tsms[i % 2]
            fload(i + 1)
            s.op("act", lambda xb=xb, c0=c0: SC.activation(out=junk, in_=xb, func=AF.Square,
                                                           accum_out=small[:, c0:c0 + 1]), [txb], [Tj, tsm])
            s.op("dve", lambda c0=c0: V.tensor_scalar(out=small[:, c0 + 1:c0 + 2], in0=small[:, c0:c0 + 1],
                                                      scalar1=1.0 / D, scalar2=EPS, op0=ALU.mult, op1=ALU.add),
                 [tsm], [tsm])
            s.op("act", lambda c0=c0: SC.activation(out=small[:, c0 + 1:c0 + 2], in_=small[:, c0 + 1:c0 + 2],
                                                    func=AF.Sqrt), [tsm], [tsm])
            s.op("dve", lambda c0=c0: V.reciprocal(out=small[:, c0 + 1:c0 + 2], in_=small[:, c0 + 1:c0 + 2]),
                 [tsm], [tsm])
            s.op("dve", lambda xb=xb, yt=yt, c0=c0: V.scalar_tensor_tensor(
                out=yt, in0=xb, scalar=small[:, c0 + 1:c0 + 2], in1=fng, op0=ALU.mult, op1=ALU.mult),
                 [txb, tsm, Tf], [ty])
            s.dma("sp", gB[4 + i % 2], yout[name][i * 128:(i + 1) * 128, :], yt, [ty], [])
        s.barrier()

    for g in GROUPS:
        for l in range(1 if fast else DEPTH):
            layer(g, l)
        final(g)
    s.barrier()
    return nc


def _consts():
    c = {}
    c["identb"] = np.eye(128, dtype=np.float32).astype(NPBF)
    c["identf"] = np.eye(128, dtype=np.float32)
    for L in (256, 1024):
        N = 2 * L
        t = np.arange(L, dtype=np.float64)[:, None]
        f = np.arange(L, dtype=np.float64)[None, :] + 0.5
        ang = 2.0 * np.pi * f * t / N
        Bm = np.concatenate([np.cos(ang), np.sin(ang)], axis=1)
        Gm = (2.0 / N) * Bm.T
        c["B%d" % L] = np.ascontiguousarray(
            Bm.reshape(L // 128, 128, N).transpose(1, 0, 2)).astype(np.float32).astype(NPBF)
        c["G%d" % L] = np.ascontiguousarray(
            Gm.reshape(N // 128, 128, L).transpose(1, 0, 2)).astype(np.float32).astype(NPBF)
        tpos = np.arange(L, dtype=np.float32)
        t_norm = tpos / np.float32(max(L - 1, 1))
        w = np.float32(2.0 * math.pi) * tpos / np.float32(L)
        bands = np.linspace(1e-4, 15, 16, dtype=np.float32)
        z = np.concatenate([t_norm[:, None], np.cos(w[:, None] * bands), -np.sin(w[:, None] * bands)], axis=-1)
        c["zp%d" % L] = np.ascontiguousarray(z.T.astype(np.float32))
        max_decay = math.log(1e-2) / 0.3
        min_decay = math.log(1e-2) / 1.5
        deltas = np.abs(np.linspace(min_decay, max_decay, 1024, dtype=np.float32))
        wn = (np.exp(-t_norm[:, None] * deltas[None, :]) + np.float32(0.05)).astype(np.float32)
        c["wn%d" % L] = np.ascontiguousarray(wn.reshape(L // 128, 128, 1024).transpose(1, 0, 2))
    tt = np.arange(1024)
    row = (tt // 64).astype(np.float32)
    col = (tt % 64).astype(np.float32)
    inv = (10000.0 ** (-np.arange(32, dtype=np.float32) / 32)).astype(np.float32)
    ang = np.concatenate([row[:, None] * inv, col[:, None] * inv], axis=-1).astype(np.float32)
    cs, sn = np.cos(ang).astype(np.float32), np.sin(ang).astype(np.float32)
    C2 = np.repeat(cs, 2, axis=1)
    c["ropeC"] = np.ascontiguousarray(C2.reshape(8, 128, 128).transpose(1, 0, 2))
    c["ropeS"] = np.ascontiguousarray(sn.reshape(8, 128, 64).transpose(1, 0, 2))
    c["ropeN"] = np.ascontiguousarray((-sn).reshape(8, 128, 64).transpose(1, 0, 2))
    return c


def _col(v):
    v = np.asarray(v, dtype=np.float32)
    sh = v.shape
    n = sh[-1] // 128
    r = v.reshape(sh[:-1] + (n, 128))
    return np.ascontiguousarray(np.moveaxis(r, -1, 0))


_NC = None


def kernel(x_prompt, x_sample, cache_k, cache_v, c, c_ctx, norm_g, w_ada, b_ada, w_in, q_norm_g, k_norm_g,
           conv_w, conv_b, filt_w1, filt_b1, filt_w2, filt_b2, filt_w3, filt_freq, hy_bias, w_out, final_norm_g):
    global _NC
    f = lambda a: np.ascontiguousarray(np.asarray(a, dtype=np.float32))
    x_prompt, x_sample, cache_k, cache_v = f(x_prompt), f(x_sample), f(cache_k), f(cache_v)
    if _NC is None:
        import os
        st = os.environ.get("KSTOP")
        _NC = build_nc(int(st) if st else None, fast=bool(os.environ.get("KFAST")))
    nc = _NC
    cst = _consts()
    shared = dict(cst)
    shared["ng"] = _col(norm_g).reshape(128, 32)
    shared["bada"] = _col(b_ada).reshape(128, 96)
    shared["cw"] = _col(f(conv_w)).reshape(128, 144)
    shared["cb"] = _col(conv_b).reshape(128, 48)
    shared["hb"] = _col(hy_bias).reshape(128, 16)
    fc = np.stack([f(filt_b1), f(filt_b2), f(filt_freq)], axis=-1)
    shared["fcol"] = np.ascontiguousarray(fc.transpose(1, 0, 2)).reshape(64, 6)
    shared["gq"] = np.ascontiguousarray(np.broadcast_to(f(q_norm_g)[None], (128, 2, 128))).reshape(128, 256)
    shared["gk"] = np.ascontiguousarray(np.broadcast_to(f(k_norm_g)[None], (128, 2, 128))).reshape(128, 256)
    shared["fng"] = np.ascontiguousarray(np.broadcast_to(f(final_norm_g)[None], (128, D)))
    import os
    if os.environ.get("KFAST"):
        shared["wada"] = f(w_ada)[:, 0:128]
        shared["win"] = f(w_in)[0:1]
        shared["wout"] = f(w_out)[0:1]
    else:
        shared["wada"] = f(w_ada)
        shared["win"] = f(w_in)
        shared["wout"] = f(w_out)
    shared["fw1"] = f(filt_w1)
    shared["fw2"] = f(filt_w2)
    shared["fw3"] = f(filt_w3)
    cc = f(c)
    cctx = f(c_ctx)
    in_maps = []
    for core in range(8):
        b = core // 4
        m = dict(shared)
        m["xp"] = x_prompt[2 * core:2 * core + 2].reshape(512, D)
        m["xs"] = x_sample[b]
        m["ck"] = cache_k[b].reshape(2, 512, 512)
        m["cv"] = cache_v[b].reshape(2, 512, 512)
        m["ccol"] = np.ascontiguousarray(np.stack([_col(cctx), _col(cc[b])], axis=1)).reshape(128, 32)
        in_maps.append(m)
    res = run_bass_kernel_spmd(nc, in_maps, core_ids=list(range(8)))
    r = res.results
    y_prompt = np.concatenate([r[i]["yp"].reshape(2, 256, D) for i in range(8)], axis=0)
    y_sample = np.stack([r[0]["ys"], r[4]["ys"]], axis=0)
    new_k = np.concatenate([r[i]["nk"].reshape(2, 2, 256, 4, 128) for i in range(8)], axis=0)
    new_v = np.concatenate([r[i]["nv"].reshape(2, 2, 256, 4, 128) for i in range(8)], axis=0)
    return (y_prompt.astype(np.float32), y_sample.astype(np.float32), new_k.astype(np.float32),
            new_v.astype(np.float32))
```

```python
import math
import numpy as np
import ml_dtypes
import concourse.bass as bass
import concourse.mybir as mybir
from concourse.bass_utils import run_bass_kernel_spmd

F32 = mybir.dt.float32
BF16 = mybir.dt.bfloat16
AF = mybir.ActivationFunctionType
ALU = mybir.AluOpType
AX = mybir.AxisListType
NPBF = ml_dtypes.bfloat16

D = 2048
DEPTH = 2
DIN = 7168
EPS = 1e-6
PI = math.pi
MAGIC = 12582912.0


class Tok:
    __slots__ = ("sem", "val", "eng", "grp")

    def __init__(self, sem, val, eng, grp=None):
        self.sem, self.val, self.eng, self.grp = sem, val, eng, grp


class T:
    def __init__(self, name=""):
        self.name = name
        self.w = None
        self.r = {}


class S:
    def __init__(self, nc):
        self.nc = nc
        self.E = {"pe": nc.tensor, "dve": nc.vector, "act": nc.scalar, "pool": nc.gpsimd, "sp": nc.sync}
        self.sem = {e: nc.alloc_semaphore(name="s_" + e) for e in self.E}
        self.cnt = {e: 0 for e in self.E}
        self.known = {e: {} for e in self.E}
        self.dsem = {}
        self.dcnt = {}
        self.ninst = 0
        self.stop = None
        self.marks = 0
        self.dead = False

    def dgroup(self, name):
        self.dsem[name] = self.nc.alloc_semaphore(name="d_" + name)
        self.dcnt[name] = 0
        return name

    def _wait(self, e, tok):
        if tok is None:
            return
        val = tok.val
        if tok.eng == "dma":
            val = 16 * self.dcnt[tok.grp]
        elif tok.eng == e and e == "pe":
            return
        key = id(tok.sem)
        if self.known[e].get(key, 0) >= val:
            return
        self.E[e].wait_ge(tok.sem, val)
        self.known[e][key] = val

    def deps(self, e, reads, writes):
        for t in reads:
            self._wait(e, t.w)
        for t in writes:
            self._wait(e, t.w)
            for r in t.r.values():
                self._wait(e, r)

    def done(self, tok, reads, writes):
        key = tok.grp if tok.eng == "dma" else tok.eng
        for t in reads:
            t.r[key] = tok
        for t in writes:
            t.w = tok
            t.r = {}

    def op(self, e, fn, reads, writes):
        if self.dead:
            return None
        self.deps(e, reads, writes)
        ins = fn()
        self.cnt[e] += 1
        ins.then_inc(self.sem[e], 1)
        tok = Tok(self.sem[e], self.cnt[e], e)
        self.done(tok, reads, writes)
        self.ninst += 1
        return tok

    def pe(self, fns, reads, writes):
        if self.dead:
            return None
        self.deps("pe", reads, writes)
        ins = None
        for fn in fns:
            ins = fn()
            self.ninst += 1
        self.cnt["pe"] += 1
        ins.then_inc(self.sem["pe"], 1)
        tok = Tok(self.sem["pe"], self.cnt["pe"], "pe")
        self.done(tok, reads, writes)
        return tok

    def dma(self, e, grp, out, in_, reads, writes):
        if self.dead:
            return None
        self.deps(e, reads, writes)
        ins = self.E[e].dma_start(out=out, in_=in_)
        self.dcnt[grp] += 1
        ins.then_inc(self.dsem[grp], 16)
        tok = Tok(self.dsem[grp], 16 * self.dcnt[grp], "dma", grp)
        self.done(tok, reads, writes)
        self.ninst += 1
        return tok

    def cut(self, n):
        import os
        c = os.environ.get("KCUT")
        if c is None or self.dead:
            return
        cn, ck = (c.split(":") + ["1"])[:2]
        if int(cn) != n:
            return
        self.cutcnt = getattr(self, "cutcnt", 0) + 1
        if self.cutcnt >= int(ck):
            print("CUT at", n)
            self.dead = True
            self.dead = False
            st, self.stop = self.stop, None
            self.barrier()
            self.stop = st
            self.dead = True

    def barrier(self):
        self.marks += 1
        if self.dead:
            return
        if self.stop is not None and self.marks >= self.stop:
            self.dead = True
        for e in self.E:
            for o in self.E:
                if o != e and self.cnt[o] > 0:
                    self._wait(e, Tok(self.sem[o], self.cnt[o], o))
            for g in self.dsem:
                if self.dcnt[g] > 0:
                    self._wait(e, Tok(self.dsem[g], 16 * self.dcnt[g], "dma", g))


GROUPS = [
    dict(name="p", nseq=2, L=256, T=512, ctx=0, rope=False, vec=0),
    dict(name="s", nseq=1, L=1024, T=1024, ctx=512, rope=True, vec=1),
]


def build_nc(stop=None, fast=False):
    nc = bass.Bass("TRN2", target_bir_lowering=False)

    def din(name, shape, dt=F32):
        return nc.dram_tensor(name, list(shape), dt, kind="ExternalInput").ap()

    def dout(name, shape):
        return nc.dram_tensor(name, list(shape), F32, kind="ExternalOutput").ap()

    def dscr(name, shape):
        return nc.dram_tensor(name, list(shape), F32, kind="Internal").ap()

    xin = {"p": din("xp", [512, D]), "s": din("xs", [1024, D])}
    ck_d = din("ck", [2, 512, 512])
    cv_d = din("cv", [2, 512, 512])
    ccol_d = din("ccol", [128, 32])
    ng_d = din("ng", [128, 32])
    bada_d = din("bada", [128, 96])
    cw_d = din("cw", [128, 144])
    cb_d = din("cb", [128, 48])
    hb_d = din("hb", [128, 16])
    fcol_d = din("fcol", [64, 6])
    gq_d = din("gq", [128, 256])
    gk_d = din("gk", [128, 256])
    fng_d = din("fng", [128, D])
    wada_d = din("wada", [2, 128 if fast else D, 3 * D])
    win_d = din("win", [1 if fast else 2, D, DIN])
    wout_d = din("wout", [1 if fast else 2, D, D])
    fw1_d = din("fw1", [2, 33, 64])
    fw2_d = din("fw2", [2, 64, 64])
    fw3_d = din("fw3", [2, 64, 2048])
    identb_d = din("identb", [128, 128], BF16)
    identf_d = din("identf", [128, 128])
    Bm_d = {256: din("B256", [128, 2, 512], BF16), 1024: din("B1024", [128, 8, 2048], BF16)}
    Gm_d = {256: din("G256", [128, 4, 256], BF16), 1024: din("G1024", [128, 16, 1024], BF16)}
    zp_d = {256: din("zp256", [33, 256]), 1024: din("zp1024", [33, 1024])}
    wn_d = {256: din("wn256", [128, 2, 1024]), 1024: din("wn1024", [128, 8, 1024])}
    rC_d = din("ropeC", [128, 8, 128])
    rS_d = din("ropeS", [128, 8, 64])
    rN_d = din("ropeN", [128, 8, 64])

    yout = {"p": dout("yp", [512, D]), "s": dout("ys", [1024, D])}
    nk_d = dout("nk", [2, 2, 256, 512])
    nv_d = dout("nv", [2, 2, 256, 512])
    xsc = {"p": dscr("xscp", [512, D]), "s": dscr("xscs", [1024, D])}
    Hs_d = {}
    for g in GROUPS:
        for l in range(DEPTH):
            Hs_d[(g["name"], l)] = dscr("Hs_%s%d" % (g["name"], l), [128, 2 * g["L"] // 128, 1024])

    def sb(name, shape, dt=F32):
        return nc.alloc_sbuf_tensor("sb_" + name, list(shape), dt)

    mixT = sb("mixT", [128, 16, 1024], BF16)
    zfm = sb("zfm", [128, 8, 1024], BF16)
    R1 = sb("R1", [128, 16384], BF16)
    R2 = sb("R2", [128, 20480], BF16)
    WB = sb("WB", [128, 16384], BF16)
    SCR = sb("SCR", [128, 16384], BF16)
    identb = sb("identb", [128, 128], BF16)
    identf = sb("identf", [128, 128])
    onesb = sb("onesb", [128, 128], BF16)
    onesf = sb("onesf", [128, 128])
    ccol = sb("ccol", [128, 32])
    scb = sb("scb", [128, 16, 2], BF16)
    modc = sb("modc", [128, 2, 48, 2])
    gsc = sb("gsc", [128, 2, 2, 16])
    ngc = sb("ngc", [128, 2, 16])
    badac = sb("badac", [128, 2, 48])
    cwc = sb("cwc", [128, 2, 72])
    cbc = sb("cbc", [128, 2, 24])
    hbc = sb("hbc", [128, 2, 8])
    fcol = sb("fcol", [64, 2, 3])
    ffb = sb("ffb", [64, 2, 2])
    gqb = sb("gqb", [128, 2, 128])
    gkb = sb("gkb", [128, 2, 128])
    WB2 = sb("WB2", [128, 8192], BF16)
    small = sb("small", [128, 64])
    Dg = sb("Dg", [128, 128])

    def view(arena, off, shape, dt):
        nb = int(np.prod(shape)) * (4 if dt == F32 else 2)
        v = arena[:, off // 2:(off + nb) // 2]
        if dt == F32:
            v = v.bitcast(F32)
        if len(shape) == 1:
            return v
        names = " ".join("a%d" % i for i in range(len(shape)))
        kw = {"a%d" % i: shape[i] for i in range(len(shape))}
        return v.rearrange("p (%s) -> p %s" % (names, names), **kw)

    PA = nc.alloc_psum_tensor("PA", [128, 1024], F32)
    PB = nc.alloc_psum_tensor("PB", [128, 1024], F32)
    PC = nc.alloc_psum_tensor("PC", [128, 1024], F32)
    PT = [nc.alloc_psum_tensor("PT0", [128, 1024], BF16), nc.alloc_psum_tensor("PT1", [128, 1024], BF16)]
    TPT = [T("pt0"), T("pt1")]
    banks = []
    for nm, P in (("a", PA), ("b", PB), ("c", PC)):
        for h in range(2):
            banks.append((P[:, h * 512:(h + 1) * 512], T(nm + str(h))))
    pairs = [(PA, banks[0][1], banks[1][1]), (PB, banks[2][1], banks[3][1]), (PC, banks[4][1], banks[5][1])]

    s = S(nc)
    s.stop = stop
    gC = s.dgroup("c")
    gW = s.dgroup("w")
    gX = s.dgroup("x")
    gO = s.dgroup("o")
    gM = s.dgroup("m")
    gB = [s.dgroup("b%d" % i) for i in range(8)]
    gK = s.dgroup("k")

    rr = {"bank": 0, "pair": 0, "pt": 0, "slot": 0}

    def nbank():
        rr["bank"] = (rr["bank"] + 1) % 6
        return banks[rr["bank"]]

    def npair():
        rr["pair"] = (rr["pair"] + 1) % 3
        return pairs[rr["pair"]]

    def npt():
        rr["pt"] = (rr["pt"] + 1) % 2
        return PT[rr["pt"]], TPT[rr["pt"]]

    V, SC, AC, PL = nc.vector, nc.scalar, nc.tensor, nc.gpsimd

    TC = T("consts")
    cl = [
        (identb[:], identb_d), (identf[:], identf_d), (ccol[:], ccol_d),
        (ngc[:], ng_d.rearrange("p (l k) -> p l k", l=2)),
        (badac[:], bada_d.rearrange("p (l k) -> p l k", l=2)),
        (cwc[:], cw_d.rearrange("p (l k) -> p l k", l=2)),
        (cbc[:], cb_d.rearrange("p (l k) -> p l k", l=2)),
        (hbc[:], hb_d.rearrange("p (l k) -> p l k", l=2)),
        (fcol[:], fcol_d.rearrange("p (l k) -> p l k", l=2)),
        (gqb[:], gq_d.rearrange("p (l k) -> p l k", l=2)),
        (gkb[:], gk_d.rearrange("p (l k) -> p l k", l=2)),
    ]
    for o, i in cl:
        s.dma("sp", gC, o, i, [], [TC])
    s.op("dve", lambda: V.memset(onesb[:], 1.0), [], [TC])
    s.op("dve", lambda: V.memset(onesf[:], 1.0), [TC], [TC])
    for l in range(2):
        for j in range(2):
            s.op("dve", lambda l=l, j=j: V.tensor_tensor(out=ffb[:, l, j:j + 1], in0=fcol[:, l, j:j + 1],
                                                         in1=fcol[:, l, 2:3], op=ALU.mult), [TC], [TC])
    s.op("act", lambda: SC.activation(out=scb[:].rearrange("p k v -> p v k"),
                                      in_=ccol[:].rearrange("p (v k) -> p v k", v=2), func=AF.Silu), [TC], [TC])
    s.barrier()

    wslots = [view(WB, 0, [16, 512], BF16), view(WB, 16384, [16, 512], BF16), view(WB2, 0, [16, 512], BF16)]
    Tslots = [T("w0"), T("w1"), T("w2")]
    gWs = [s.dgroup("w0"), s.dgroup("w1"), s.dgroup("w2")]

    def load_w(src):
        rr["slot"] = (rr["slot"] + 1) % 3
        sl, ts = wslots[rr["slot"]], Tslots[rr["slot"]]
        s.dma("pool", gWs[rr["slot"]], sl, src.rearrange("(k p) c -> p k c", p=128), [], [ts])
        return sl, ts

    def stream(blocks):
        q = []
        nx = 0
        while nx < min(2, len(blocks)):
            q.append(load_w(blocks[nx]))
            nx += 1
        for i in range(len(blocks)):
            cur = q.pop(0)
            if nx < len(blocks):
                q.append(load_w(blocks[nx]))
                nx += 1
            yield i, cur[0], cur[1]

    class WStream:
        def __init__(self, blocks):
            self.blocks, self.q, self.nx = blocks, [], 0

        def prime(self, depth=2):
            while len(self.q) < depth and self.nx < len(self.blocks):
                self.q.append(load_w(self.blocks[self.nx]))
                self.nx += 1

        def take(self, n):
            for i in range(n):
                self.prime(1)
                cur = self.q.pop(0)
                self.prime(2)
                yield i, cur[0], cur[1]

    FT = {"wn": [T(), T()], "hst": [T(), T()], "f": T(), "t": T(), "h": T(), "B": T(),
          "w3": T(), "w3b": T(), "zp": T(), "w1": T(), "w2": T()}

    def filter_phase(g, l):
        L = g["L"]
        NT = L // 128
        NR = 2 * NT
        Bm = view(R1, 0, [NT, 2 * L], BF16)
        hp = view(R2, 0, [NT, 1024], BF16)
        hm = view(R2, NT * 2048, [NT, 1024], BF16)
        MX = mixT[:].rearrange("p a b -> p (a b)")
        fw3 = view(MX, 0, [2048], F32)
        zp = view(MX, 8192, [L], F32)
        h1 = view(MX, 12288, [L], F32)
        ZF = zfm[:].rearrange("p a b -> p (a b)")
        h2 = view(ZF, 4096, [L], BF16)
        fw3b = view(ZF, 0, [2048], BF16)
        aa = view(MX, 20480, [L], F32)
        kt = view(MX, 25600, [L], F32)
        fw1 = view(MX, 24576, [64], F32)
        fw2 = view(MX, 24576 + 256, [64], F32)
        wnt = [view(SCR, 0, [1024], F32), view(SCR, 4096, [1024], F32)]
        Hst = [view(SCR, 8192, [1024], F32), view(SCR, 12288, [1024], F32)]
        t1 = view(SCR, 16384, [512], F32)
        t2 = view(SCR, 18432, [512], F32)
        Twn, THst, Tf, Tt, Th, TB = FT["wn"], FT["hst"], FT["f"], FT["t"], FT["h"], FT["B"]
        if FT.get("BL") != L:
            s.dma("sp", gM, Bm, Bm_d[L], [], [TB])
            FT["BL"] = L
        Tw3, Tw3b, Tzp, Tw1, Tw2 = FT["w3"], FT["w3b"], FT["zp"], FT["w1"], FT["w2"]
        s.dma("sp", gB[2], fw3[0:64, :], fw3_d[l], [], [Tw3])
        s.dma("sp", gB[3], zp[0:33, :], zp_d[L], [], [Tzp])
        s.dma("sp", gB[6], fw1[0:33, :], fw1_d[l], [], [Tw1])
        s.dma("sp", gB[7], fw2[0:64, :], fw2_d[l], [], [Tw2])
        s.op("dve", lambda: V.tensor_copy(out=fw3b[0:64, :], in_=fw3[0:64, :]), [Tw3], [Tw3b])
        n = min(L, 512)
        for stage in range(2):
            src = zp if stage == 0 else h1
            dst = h1 if stage == 0 else h2
            wt = fw1 if stage == 0 else fw2
            kk = 33 if stage == 0 else 64
            for sp in range(L // n):
                ps, tp = nbank()
                s.pe([lambda ps=ps, wt=wt, src=src, sp=sp, kk=kk: AC.matmul(ps[0:64, 0:n], wt[0:kk, :],
                                                                            src[0:kk, sp * n:(sp + 1) * n],
                                                                            start=True, stop=True)],
                     [Tzp, Tw1] if stage == 0 else [Tw2, Tf], [tp])
                s.op("dve", lambda ps=ps, sp=sp, stage=stage: V.tensor_scalar(
                    out=aa[0:64, sp * n:(sp + 1) * n], in0=ps[0:64, 0:n], scalar1=fcol[:, l, 2:3],
                    scalar2=ffb[:, l, stage:stage + 1], op0=ALU.mult, op1=ALU.add), [tp, TC], [Tf])
                asl = aa[0:64, sp * n:(sp + 1) * n]
                ksl = kt[0:64, sp * n:(sp + 1) * n]
                s.op("dve", lambda asl=asl, ksl=ksl: V.tensor_scalar(
                    out=ksl, in0=asl, scalar1=1.0 / (2.0 * PI), scalar2=MAGIC, op0=ALU.mult, op1=ALU.add),
                     [Tf], [Tf])
                s.op("dve", lambda ksl=ksl: V.tensor_scalar(
                    out=ksl, in0=ksl, scalar1=MAGIC, scalar2=2.0 * PI, op0=ALU.subtract, op1=ALU.mult),
                     [Tf], [Tf])
                s.op("dve", lambda asl=asl, ksl=ksl: V.tensor_tensor(out=asl, in0=asl, in1=ksl, op=ALU.subtract),
                     [Tf], [Tf])
                s.op("dve", lambda asl=asl: V.tensor_scalar(
                    out=asl, in0=asl, scalar1=-PI, scalar2=PI, op0=ALU.max, op1=ALU.min), [Tf], [Tf])
                s.op("act", lambda sp=sp, dst=dst: SC.activation(out=dst[0:64, sp * n:(sp + 1) * n],
                                                                 in_=aa[0:64, sp * n:(sp + 1) * n], func=AF.Sin),
                     [Tf], [Tf])
                yield
        for tt in range(NT):
            wb, twb = wnt[tt % 2], Twn[tt % 2]
            s.dma("sp", gB[tt % 2], wb, wn_d[L][:, tt, :], [], [twb])
            for cf in range(2):
                psf, tpf = nbank()
                psb, tpb = nbank()
                s.pe([lambda psf=psf, tt=tt, cf=cf: AC.matmul(psf, h2[0:64, tt * 128:(tt + 1) * 128],
                                                              fw3b[0:64, cf * 512:(cf + 1) * 512], start=True,
                                                              stop=True)], [Tf, Tw3b], [tpf])
                s.pe([lambda psb=psb, tt=tt, cf=cf: AC.matmul(psb, h2[0:64, tt * 128:(tt + 1) * 128],
                                                              fw3b[0:64, 1024 + cf * 512:1024 + (cf + 1) * 512],
                                                              start=True, stop=True)], [Tf, Tw3b], [tpb])
                wsl = wb[:, cf * 512:(cf + 1) * 512]
                s.op("dve", lambda psf=psf, wsl=wsl: V.tensor_tensor(out=t1, in0=psf, in1=wsl, op=ALU.mult),
                     [tpf, twb], [Tt])
                s.op("dve", lambda psb=psb, wsl=wsl: V.tensor_tensor(out=t2, in0=psb, in1=wsl, op=ALU.mult),
                     [tpb, twb, Tt], [Tt])
                s.op("dve", lambda tt=tt, cf=cf: V.tensor_tensor(out=hp[:, tt, cf * 512:(cf + 1) * 512], in0=t1,
                                                                 in1=t2, op=ALU.add), [Tt], [Th])
                s.op("dve", lambda tt=tt, cf=cf: V.tensor_tensor(out=hm[:, tt, cf * 512:(cf + 1) * 512], in0=t1,
                                                                 in1=t2, op=ALU.subtract), [Tt], [Th, Tt])
            yield
        for rc in range(NR):
            hsrc = hp if rc < NT else hm
            hb_, thb = Hst[rc % 2], THst[rc % 2]
            for half in range(2):
                ps, tp = nbank()
                s.pe([lambda ps=ps, tc=tc, rc=rc, half=half, hsrc=hsrc: AC.matmul(
                    ps, Bm[:, tc, rc * 128:(rc + 1) * 128], hsrc[:, tc, half * 512:(half + 1) * 512],
                    start=(tc == 0), stop=(tc == NT - 1)) for tc in range(NT)], [TB, Th], [tp])
                s.op("act", lambda ps=ps, half=half, hb_=hb_: SC.copy(out=hb_[:, half * 512:(half + 1) * 512],
                                                                      in_=ps), [tp], [thb])
            s.dma("sp", gB[4 + rc % 2], Hs_d[(g["name"], l)][:, rc, :], hb_, [thb], [])
            yield

    def all_filters():
        for g in GROUPS[::-1]:
            for l in range(DEPTH):
                if not fast:
                    yield from filter_phase(g, l)

    Tmod = T("mod")
    filt_holder = [all_filters()]
    modr = [view(SCR, 20480, [512], F32), view(SCR, 22528, [512], F32)]
    Tmr = [T(), T()]
    apend = []

    def pull(nsteps):
        if not filt_holder:
            return
        for _ in range(nsteps):
            try:
                next(filt_holder[0])
            except StopIteration:
                return

    if fast:
        s.op("dve", lambda: V.memset(modc[:], 0.5), [], [Tmod])
    for l in range(0 if fast else DEPTH):
        blocks = [wada_d[l][:, b * 512:(b + 1) * 512] for b in range(12)]
        for b, sl, ts in stream(blocks):
            pull(3)
            ps, tp = nbank()
            s.pe([lambda k=k, sl=sl, ps=ps: AC.matmul(ps[0:2, :], scb[:, k, :], sl[:, k, :], start=(k == 0),
                                                      stop=(k == 15)) for k in range(16)], [ts, TC], [tp])
            mr, tmr = modr[b % 2], Tmr[b % 2]
            s.op("act", lambda ps=ps, mr=mr: SC.copy(out=mr[0:2, :], in_=ps[0:2, :]), [tp], [tmr])

            def flips(mr=mr, tmr=tmr, l=l, b=b):
                ps2, tp2 = nbank()
                s.pe([lambda jj=jj: AC.matmul(ps2[:, jj * 2:(jj + 1) * 2], mr[0:2, jj * 128:(jj + 1) * 128],
                                              identf[0:2, 0:2], start=True, stop=True) for jj in range(4)],
                     [tmr, TC], [tp2])
                bsl = badac[:, l, b * 4:(b + 1) * 4].rearrange("p (j o) -> p j o", o=1).to_broadcast([128, 4, 2])
                s.op("dve", lambda: V.tensor_tensor(
                    out=modc[:, l, b * 4:(b + 1) * 4, :], in0=ps2[:, 0:8].rearrange("p (j v) -> p j v", v=2),
                    in1=bsl, op=ALU.add), [tp2, TC], [Tmod])

            while apend:
                apend.pop(0)()
            apend.append(flips)
        while apend:
            apend.pop(0)()
    for l in range(DEPTH):
        for v in range(2):
            s.op("dve", lambda l=l, v=v: V.tensor_scalar(out=gsc[:, l, v, :], in0=modc[:, l, 16:32, v], scalar1=1.0,
                                                         scalar2=None, op0=ALU.add), [Tmod], [Tmod])
            s.op("dve", lambda l=l, v=v: V.tensor_tensor(out=gsc[:, l, v, :], in0=gsc[:, l, v, :], in1=ngc[:, l, :],
                                                         op=ALU.mult), [Tmod, TC], [Tmod])
    pull(100000)
    s.barrier()

    def layer(g, l):
        name, L, T_, nseq, ctx, vec = g["name"], g["L"], g["T"], g["nseq"], g["ctx"], g["vec"]
        NTI = T_ // 128
        NT = L // 128
        NR = 2 * NT
        xsrc = xin[name] if l == 0 else xsc[name]
        hT = view(R1, 0, [16, 1024], BF16)
        qT = view(R2, 0, [8, 1024], BF16)
        kT = view(R2, 16384, [4, 1536], BF16)
        Vt = view(R2, 16384 + 12288, [12, 512], BF16)
        TqT, TkT, TV, Tmix, Tz = T("qT"), T("kT"), T("V"), T("mix"), T("z")
        ThTk = [T("hT%d" % k_) for k_ in range(16)]

        kinds = [("ga", 2048), ("ga", 2560), ("gh", 6144), ("gh", 6656), ("x0", 3072), ("x0", 3584),
                 ("x1", 4096), ("x1", 4608), ("vv", 5120), ("vv", 5632)]
        ws = WStream([win_d[l][:, b * 512:(b + 1) * 512] for b in range(4)] +
                     [win_d[l][:, c0:c0 + 512] for _, c0 in kinds])
        ws.prime(2)
        xt = [view(SCR, 0, [2048], F32), view(SCR, 8192, [2048], F32)]
        Txt = [T(), T()]
        xnb = [view(SCR, 16384, [2048], BF16), view(SCR, 20480, [2048], BF16)]
        Txn = [T(), T()]
        junk = view(SCR, 24576, [2048], BF16)
        Tjunk = T()
        Tsa = [T(), T(), T()]
        qB, qC = [], []
        pts8 = [(PT[0], TPT[0]), (PT[1], TPT[1])] + [(bk.bitcast(BF16), tb) for bk, tb in banks]
        ptr = [0]

        def npt8():
            ptr[0] = (ptr[0] + 1) % 8
            return pts8[ptr[0]]

        for i in range(NTI):
            xb, txb = xt[i % 2], Txt[i % 2]
            xn, txn = xnb[i % 2], Txn[i % 2]
            ca = 16 + 2 * (i % 3)
            tsm = Tsa[i % 3]
            while qC:
                qC.pop(0)()
            s.dma("sp", gB[i % 2], xb, xsrc[i * 128:(i + 1) * 128, :], [], [txb])
            s.op("act", lambda xb=xb, ca=ca: SC.activation(out=junk, in_=xb, func=AF.Square,
                                                           accum_out=small[:, ca:ca + 1]), [txb], [Tjunk, tsm])
            s.op("dve", lambda ca=ca: V.tensor_scalar(out=small[:, ca + 1:ca + 2], in0=small[:, ca:ca + 1],
                                                      scalar1=1.0 / D, scalar2=EPS, op0=ALU.mult, op1=ALU.add),
                 [tsm], [tsm])
            s.op("act", lambda ca=ca: SC.activation(out=small[:, ca + 1:ca + 2], in_=small[:, ca + 1:ca + 2],
                                                    func=AF.Sqrt), [tsm], [tsm])

            def stageB(xb=xb, txb=txb, xn=xn, txn=txn, ca=ca, tsm=tsm, i=i):
                s.op("dve", lambda: V.reciprocal(out=small[:, ca + 1:ca + 2], in_=small[:, ca + 1:ca + 2]),
                     [tsm], [tsm])
                s.op("act", lambda: SC.activation(out=xn, in_=xb, func=AF.Identity, scale=small[:, ca + 1:ca + 2]),
                     [txb, tsm], [txn])

                def stageC():
                    for q in range(4):
                        pt, tpt = npt8()
                        s.pe([lambda pt=pt, q=q, kk=kk: AC.transpose(
                            pt[:, kk * 128:(kk + 1) * 128], xn[:, (q * 4 + kk) * 128:(q * 4 + kk + 1) * 128],
                            identb[:]) for kk in range(4)], [txn, TC], [tpt])
                        for kk in range(4):
                            k = q * 4 + kk
                            if q != 3:
                                s.op("dve", lambda pt=pt, k=k, kk=kk: V.tensor_scalar(
                                    out=hT[:, k, i * 128:(i + 1) * 128], in0=pt[:, kk * 128:(kk + 1) * 128],
                                    scalar1=gsc[:, l, vec, k:k + 1], scalar2=modc[:, l, k, vec:vec + 1],
                                    op0=ALU.mult, op1=ALU.add), [tpt, Tmod], [ThTk[k]])
                            else:
                                s.op("act", lambda pt=pt, k=k, kk=kk: SC.activation(
                                    out=hT[:, k, i * 128:(i + 1) * 128], in_=pt[:, kk * 128:(kk + 1) * 128],
                                    func=AF.Identity, scale=gsc[:, l, vec, k:k + 1],
                                    bias=modc[:, l, k, vec:vec + 1]), [tpt, Tmod], [ThTk[k]])

                qC.append(stageC)

            while qB:
                qB.pop(0)()
            qB.append(stageB)
        while qB:
            qB.pop(0)()
        while qC:
            qC.pop(0)()
        s.barrier()

        sets = []
        for offs, cols in (((0, 2048, 4096, 6144, 8192, 9216), (4, 8)),
                           ((12288, 14336, 24576, 26624, 28672, 29696), (8, 12))):
            sets.append(dict(sqt=view(SCR, offs[0], [512], F32), qn=view(SCR, offs[1], [512], F32),
                             rt=view(SCR, offs[2], [512], F32), ut=view(SCR, offs[3], [512], F32),
                             qb=view(SCR, offs[4], [512], BF16), vf=view(SCR, offs[5], [512], F32),
                             sm=small[:, cols[0]:cols[1]], c0=cols[0],
                             Tsq=T(), Tqn=T(), Trt=T(), Tut=T(), Tqb=T(), Tvf=T(), Tsm=T()))
        ckb = view(SCR, 12288, [4, 512], BF16)
        Tck = T()
        ropeC = view(SCR, 16384, [8, 128], F32)
        ropeS = view(SCR, 20480, [8, 64], F32)
        ropeN = view(SCR, 22528, [8, 64], F32)
        Trope = T()
        if g["rope"]:
            s.dma("sp", gM, ropeC, rC_d, [], [Trope])
            s.dma("sp", gM, ropeS, rS_d, [], [Trope])
            s.dma("sp", gM, ropeN, rN_d, [], [Trope])
        if ctx:
            s.dma("pool", gK, ckb, ck_d[l].rearrange("(j p) c -> p j c", p=128), [], [Tck])
            s.dma("pool", gK, Vt[:, 8:12, :], cv_d[l].rearrange("(j p) c -> p j c", p=128), [], [TV])
            for j in range(4):
                pt, tpt = npt()
                s.pe([lambda pt=pt, j=j, h=h: AC.transpose(pt[:, h * 128:(h + 1) * 128],
                                                           ckb[:, j, h * 128:(h + 1) * 128], identb[:])
                      for h in range(4)], [Tck, TC], [tpt])
                s.op("act", lambda pt=pt, j=j: SC.copy(out=kT[:, :, 1024 + j * 128:1024 + (j + 1) * 128],
                                                       in_=pt[:, 0:512].rearrange("p (h d) -> p h d", h=4)),
                     [tpt], [TkT])
        qB, qC = [], []
        Tsm3 = [T(), T(), T()]
        for b, sl, ts in ws.take(4):
            for i in range(NTI):
                bs = sets[(b * NTI + i) % 2]
                ps, tp = nbank()
                s.pe([lambda ps=ps, k=k, i=i, sl=sl: AC.matmul(ps, hT[:, k, i * 128:(i + 1) * 128], sl[:, k, :],
                                                               start=(k == 0), stop=(k == 15)) for k in range(16)],
                     ThTk + [ts], [tp])
                if b == 3:
                    vf, Tvf = bs["vf"], bs["Tvf"]
                    if not ctx:
                        s.op("act", lambda ps=ps, vf=vf: SC.copy(out=vf, in_=ps), [tp], [Tvf])
                        s.dma("sp", gB[6 + (b * NTI + i) % 2], nv_d[i // 2, l, (i % 2) * 128:(i % 2 + 1) * 128, :],
                              vf, [Tvf], [])
                        s.op("dve", lambda i=i, vf=vf: V.tensor_copy(out=Vt[:, i, :], in_=vf), [Tvf], [TV])
                    else:
                        s.op("act", lambda ps=ps, i=i: SC.copy(out=Vt[:, i, :], in_=ps), [tp], [TV])
                    while qC:
                        qC.pop(0)()
                    while qB:
                        qB.pop(0)()
                    continue
                sqt, Tsq = bs["sqt"], bs["Tsq"]
                c0 = 4 + 4 * ((b * NTI + i) % 3)
                smv, Tsm = small[:, c0:c0 + 4], Tsm3[(b * NTI + i) % 3]
                for h in range(4):
                    s.op("act", lambda ps=ps, h=h, sqt=sqt, c0=c0: SC.activation(
                        out=sqt[:, h * 128:(h + 1) * 128], in_=ps[:, h * 128:(h + 1) * 128], func=AF.Square,
                        accum_out=small[:, c0 + h:c0 + h + 1]), [tp], [Tsq, Tsm])
                s.op("dve", lambda smv=smv: V.tensor_scalar(out=smv, in0=smv, scalar1=1.0 / 128, scalar2=EPS,
                                                            op0=ALU.mult, op1=ALU.add), [Tsm], [Tsm])
                s.op("act", lambda smv=smv: SC.activation(out=smv, in_=smv, func=AF.Sqrt), [Tsm], [Tsm])

                def stageB(bs=bs, ps=ps, tp=tp, b=b, i=i, smv=smv, c0=c0, Tsm=Tsm):
                    qn, rt, ut, qb = bs["qn"], bs["rt"], bs["ut"], bs["qb"]
                    Tqn, Trt, Tut, Tqb = bs["Tqn"], bs["Trt"], bs["Tut"], bs["Tqb"]
                    gb = gqb if b < 2 else gkb
                    s.op("dve", lambda: V.reciprocal(out=smv, in_=smv), [Tsm], [Tsm])
                    for h in range(4):
                        s.op("dve", lambda h=h: V.scalar_tensor_tensor(
                            out=qn[:, h * 128:(h + 1) * 128], in0=ps[:, h * 128:(h + 1) * 128],
                            scalar=small[:, c0 + h:c0 + h + 1], in1=gb[:, l, :], op0=ALU.mult, op1=ALU.mult),
                             [tp, Tsm, TC], [Tqn])
                    if b == 2 and not ctx:
                        s.dma("sp", gB[4 + (b * NTI + i) % 2], nk_d[i // 2, l, (i % 2) * 128:(i % 2 + 1) * 128, :],
                              qn, [Tqn], [])
                    if g["rope"]:
                        q4 = qn.rearrange("p (h i two) -> p h i two", h=4, two=2)
                        u4 = ut.rearrange("p (h i two) -> p h i two", h=4, two=2)
                        Cb = ropeC[:, i:i + 1, :].to_broadcast([128, 4, 128])
                        Sb = ropeS[:, i:i + 1, :].to_broadcast([128, 4, 64])
                        Nb = ropeN[:, i:i + 1, :].to_broadcast([128, 4, 64])
                        s.op("dve", lambda: V.tensor_tensor(out=rt.rearrange("p (h d) -> p h d", h=4),
                                                            in0=qn.rearrange("p (h d) -> p h d", h=4), in1=Cb,
                                                            op=ALU.mult), [Tqn, Trope], [Trt])
                        s.op("dve", lambda: V.tensor_tensor(out=u4[:, :, :, 0], in0=q4[:, :, :, 1], in1=Nb,
                                                            op=ALU.mult), [Tqn, Trope], [Tut])
                        s.op("dve", lambda: V.tensor_tensor(out=u4[:, :, :, 1], in0=q4[:, :, :, 0], in1=Sb,
                                                            op=ALU.mult), [Tqn, Trope], [Tut])
                        s.op("dve", lambda: V.tensor_tensor(out=qb, in0=rt, in1=ut, op=ALU.add), [Trt, Tut], [Tqb])
                    else:
                        s.op("act", lambda: SC.copy(out=qb, in_=qn), [Tqn], [Tqb])
                    if b < 2:
                        dst, td = qT[:, b * 4:(b + 1) * 4, i * 128:(i + 1) * 128], TqT
                    else:
                        dst, td = kT[:, :, i * 128:(i + 1) * 128], TkT

                    def fin():
                        pt, tpt = npt()
                        s.pe([lambda pt=pt, h=h: AC.transpose(pt[:, h * 128:(h + 1) * 128],
                                                              qb[:, h * 128:(h + 1) * 128], identb[:])
                              for h in range(4)], [Tqb, TC], [tpt])
                        s.op("act", lambda pt=pt: SC.copy(
                            out=dst, in_=pt[:, 0:512].rearrange("p (h d) -> p h d", h=4)), [tpt], [td])

                    qC.append(fin)

                while qC:
                    qC.pop(0)()
                while qB:
                    qB.pop(0)()
                qB.append(stageB)
        while qB:
            qB.pop(0)()
        while qC:
            qC.pop(0)()
        s.barrier()

        tmp = [view(SCR, 0, [1024], F32), view(SCR, 4096, [1024], F32)]
        Ttmp = [T(), T()]
        nsp = T_ // 512
        cnt = 0
        for b, sl, ts in ws.take(10):
            kind = kinds[b][0]
            for jj in range(4):
                c = (b % 2) * 4 + jj
                P, ta, tb = npair()
                tps = [ta, tb][:nsp]
                for sp in range(nsp):
                    s.pe([lambda P=P, sp=sp, k=k, jj=jj, sl=sl: AC.matmul(
                        P[:, sp * 512:(sp + 1) * 512], sl[:, k, jj * 128:(jj + 1) * 128],
                        hT[:, k, sp * 512:(sp + 1) * 512], start=(k == 0), stop=(k == 15)) for k in range(16)],
                         ThTk + [ts], [tps[sp]])
                if kind in ("ga", "gh"):
                    mc = c if kind == "ga" else 8 + c
                    for sp in range(nsp):
                        s.op("act", lambda P=P, sp=sp, mc=mc: SC.activation(
                            out=mixT[:, mc, sp * 512:(sp + 1) * 512], in_=P[:, sp * 512:(sp + 1) * 512],
                            func=AF.Silu), [tps[sp]], [Tmix])
                    continue
                jc = {"x0": 0, "x1": 8, "vv": 16}[kind] + c
                cnt += 1
                tm, ttm = tmp[cnt % 2], Ttmp[cnt % 2]
                s.op("act", lambda P=P, tm=tm, jc=jc: SC.activation(
                    out=tm[:, 0:T_], in_=P[:, 0:T_], func=AF.Identity, scale=cwc[:, l, 24 + jc:25 + jc],
                    bias=cbc[:, l, jc:jc + 1]), tps + [TC], [ttm])
                for sq in range(nseq):
                    s0 = sq * L
                    s.op("dve", lambda P=P, tm=tm, jc=jc, s0=s0: V.scalar_tensor_tensor(
                        out=tm[:, s0 + 1:s0 + L], in0=P[:, s0:s0 + L - 1], scalar=cwc[:, l, jc:jc + 1],
                        in1=tm[:, s0 + 1:s0 + L], op0=ALU.mult, op1=ALU.add), tps + [TC, ttm], [ttm])
                    s.op("dve", lambda P=P, tm=tm, jc=jc, s0=s0: V.scalar_tensor_tensor(
                        out=tm[:, s0:s0 + L - 1], in0=P[:, s0 + 1:s0 + L], scalar=cwc[:, l, 48 + jc:49 + jc],
                        in1=tm[:, s0:s0 + L - 1], op0=ALU.mult, op1=ALU.add), tps + [TC, ttm], [ttm])
                if kind == "x0":
                    s.op("dve", lambda tm=tm, c=c: V.tensor_tensor(out=mixT[:, 8 + c, 0:T_], in0=tm[:, 0:T_],
                                                                   in1=mixT[:, 8 + c, 0:T_], op=ALU.mult),
                         [ttm, Tmix], [Tmix])
                elif kind == "x1":
                    s.op("act", lambda tm=tm, c=c: SC.copy(out=zfm[:, c, 0:T_], in_=tm[:, 0:T_]), [ttm], [Tz])
                else:
                    s.op("dve", lambda tm=tm, c=c: V.tensor_tensor(out=zfm[:, c, 0:T_], in0=tm[:, 0:T_],
                                                                   in1=zfm[:, c, 0:T_], op=ALU.mult),
                         [ttm, Tz], [Tz])
        s.barrier()

        Gm = view(R1, 0, [NR, L], BF16)
        Hh = view(WB, 0, [NR, 512], F32)
        TGm, THh = T(), T()
        s.dma("sp", gM, Gm, Gm_d[L], [], [TGm])
        wf = WStream([wout_d[l][:, b * 512:(b + 1) * 512] for b in range(4)])
        if L == 256:
            rr["slot"] = 0
            wf.prime(2)
        else:
            rr["slot"] = 1
            wf.prime(1)
        Hfull = view(WB, 0, [NR, 1024], F32) if L == 256 else None
        if L == 256:
            s.dma("sp", gM, Hfull, Hs_d[(name, l)], [], [THh])
        else:
            s.dma("sp", gM, Hh, Hs_d[(name, l)][:, :, 0:512], [], [THh])
        Eb = [view(SCR, j * 1024, [512], BF16) for j in range(4)]
        TE = [T() for _ in range(4)]
        rd = view(SCR, 4096, [512], F32)
        ov = view(SCR, 6144, [512], F32)
        Trd, Tov = T(), T()
        nkc = (L + ctx) // 128
        sm_scale = 1.0 / math.sqrt(128.0)
        ecnt = 0
        acnt = 0
        for sq in range(nseq):
            s0 = sq * L
            for gi in range(4):
                for span in range(L // 256):
                    q0 = s0 + span * 256
                    qv = qT[:, 2 * gi:2 * gi + 2, q0:q0 + 256]
                    acnt += 1
                    psS = [banks[0], banks[1]]
                    psO, tO = banks[2 + 2 * (acnt % 2)]
                    psD, tD = banks[3 + 2 * (acnt % 2)]

                    def smm(kc):
                        ps, tp = psS[kc % 2]
                        if ctx:
                            koff = kc * 128
                        else:
                            koff = s0 + kc * 128
                        s.pe([lambda ps=ps, koff=koff: AC.matmul(ps, kT[:, gi, koff:koff + 128], qv, start=True,
                                                                 stop=True)], [TkT, TqT], [tp])

                    smm(0)
                    for kc in range(nkc):
                        if kc + 1 < nkc:
                            smm(kc + 1)
                        ps, tp = psS[kc % 2]
                        ecnt += 1
                        E, tE = Eb[ecnt % 4], TE[ecnt % 4]
                        s.op("act", lambda ps=ps, E=E: SC.activation(out=E, in_=ps, func=AF.Exp, scale=sm_scale),
                             [tp], [tE])
                        vi = kc if ctx else sq * 2 + kc
                        s.pe([lambda E=E, vi=vi, kc=kc: AC.matmul(psO, Vt[:, vi, gi * 128:(gi + 1) * 128], E,
                                                                  start=(kc == 0), stop=(kc == nkc - 1)),
                              lambda E=E, kc=kc: AC.matmul(psD, onesb[:], E, start=(kc == 0),
                                                           stop=(kc == nkc - 1))], [TV, tE, TC], [tO, tD])
                    s.op("dve", lambda: V.reciprocal(out=rd, in_=psD), [tD], [Trd])
                    s.op("dve", lambda: V.tensor_tensor(out=ov, in0=psO, in1=rd, op=ALU.mult), [tO, Trd], [Tov])
                    mv = mixT[:, 2 * gi:2 * gi + 2, q0:q0 + 256]
                    s.op("dve", lambda mv=mv: V.tensor_tensor(out=mv, in0=ov.rearrange("p (h q) -> p h q", h=2),
                                                              in1=mv, op=ALU.mult), [Tov, Tmix], [Tmix])
        s.barrier()

        Bm = view(R2, 0, [NT, 2 * L], BF16)
        ztm = view(R2, 32768, [NT, 512], BF16)
        Yr = view(SCR, 0, [NR, 512], BF16)
        zc = view(SCR, 16384, [512], F32)
        zs = view(SCR, 18432, [512], F32)
        t1 = view(SCR, 20480, [512], F32)
        t2 = view(SCR, 22528, [512], F32)
        ty = view(SCR, 24576, [1024], F32)
        TBm, Tzt, TYr, Tzc, Tzs, Tt1, Tt2, Tty = (T() for _ in range(8))
        s.dma("sp", gM, Bm, Bm_d[L], [], [TBm])
        n = min(L, 512)
        for sq in range(nseq):
            s0 = sq * L
            for half in range(2):
                if L != 256 and not (sq == 0 and half == 0):
                    s.dma("sp", gM, Hh, Hs_d[(name, l)][:, :, half * 512:(half + 1) * 512], [], [THh])
                for tt in range(NT):
                    pt, tpt = npt()
                    s.pe([lambda pt=pt, cc=cc, tt=tt: AC.transpose(
                        pt[:, cc * 128:(cc + 1) * 128],
                        zfm[:, half * 4 + cc, s0 + tt * 128:s0 + (tt + 1) * 128], identb[:]) for cc in range(4)],
                         [Tz, TC], [tpt])
                    s.op("act", lambda pt=pt, tt=tt: SC.copy(out=ztm[:, tt, :], in_=pt[:, 0:512]), [tpt], [Tzt])
                for j in range(NT):
                    psc, tpc = banks[(j % 2) * 2]
                    pss, tps_ = banks[(j % 2) * 2 + 1]
                    s.pe([lambda psc=psc, tc=tc, j=j: AC.matmul(psc, Bm[:, tc, j * 128:(j + 1) * 128],
                                                                ztm[:, tc, :], start=(tc == 0),
                                                                stop=(tc == NT - 1)) for tc in range(NT)],
                         [TBm, Tzt], [tpc])
                    s.pe([lambda pss=pss, tc=tc, j=j: AC.matmul(pss, Bm[:, tc, (NT + j) * 128:(NT + j + 1) * 128],
                                                                ztm[:, tc, :], start=(tc == 0),
                                                                stop=(tc == NT - 1)) for tc in range(NT)],
                         [TBm, Tzt], [tps_])
                    s.op("act", lambda psc=psc: SC.copy(out=zc, in_=psc), [tpc], [Tzc])
                    s.op("act", lambda pss=pss: SC.copy(out=zs, in_=pss), [tps_], [Tzs])
                    if L == 256:
                        Hc = Hfull[:, j, half * 512:(half + 1) * 512]
                        Hs_ = Hfull[:, NT + j, half * 512:(half + 1) * 512]
                    else:
                        Hc, Hs_ = Hh[:, j, :], Hh[:, NT + j, :]
                    s.op("dve", lambda Hc=Hc: V.tensor_tensor(out=t1, in0=zc, in1=Hc, op=ALU.mult),
                         [Tzc, THh], [Tt1])
                    s.op("dve", lambda Hs_=Hs_: V.tensor_tensor(out=t2, in0=zs, in1=Hs_, op=ALU.mult),
                         [Tzs, THh], [Tt2])
                    s.op("dve", lambda j=j: V.tensor_tensor(out=Yr[:, j, :], in0=t1, in1=t2, op=ALU.subtract),
                         [Tt1, Tt2], [TYr])
                    s.op("dve", lambda Hs_=Hs_: V.tensor_tensor(out=t1, in0=zc, in1=Hs_, op=ALU.mult),
                         [Tzc, THh], [Tt1])
                    s.op("dve", lambda Hc=Hc: V.tensor_tensor(out=t2, in0=zs, in1=Hc, op=ALU.mult),
                         [Tzs, THh], [Tt2])
                    s.op("dve", lambda j=j: V.tensor_tensor(out=Yr[:, NT + j, :], in0=t1, in1=t2, op=ALU.add),
                         [Tt1, Tt2], [TYr])
                for cc in range(4):
                    c = half * 4 + cc
                    P, ta, tb = pairs[2]
                    tps = [ta, tb][:L // n]
                    for sp in range(L // n):
                        s.pe([lambda sp=sp, rc=rc, cc=cc: AC.matmul(
                            P[:, sp * 512:sp * 512 + n], Yr[:, rc, cc * 128:(cc + 1) * 128],
                            Gm[:, rc, sp * n:(sp + 1) * n], start=(rc == 0), stop=(rc == NR - 1))
                              for rc in range(NR)], [TYr, TGm], [tps[sp]])
                    s.op("dve", lambda c=c: V.scalar_tensor_tensor(
                        out=ty[:, 0:L], in0=zfm[:, c, s0:s0 + L], scalar=hbc[:, l, c:c + 1], in1=P[:, 0:L],
                        op0=ALU.mult, op1=ALU.add), tps + [Tz, TC], [Tty])
                    s.op("dve", lambda c=c: V.tensor_tensor(out=mixT[:, 8 + c, s0:s0 + L], in0=ty[:, 0:L],
                                                            in1=mixT[:, 8 + c, s0:s0 + L], op=ALU.mult),
                         [Tty, Tmix], [Tmix])
        s.barrier()

        gbc = view(SCR, 0, [2048], F32)
        xs_ = [view(SCR, 8192, [512], F32), view(SCR, 10240, [512], F32),
               view(SCR, 18432, [512], F32), view(SCR, 20480, [512], F32)]
        xo = [view(SCR, 12288, [512], F32), view(SCR, 14336, [512], F32),
              view(SCR, 22528, [512], F32), view(SCR, 24576, [512], F32)]
        to = view(SCR, 16384, [512], F32)
        Tg, Tto, TDg = T(), T(), T()
        Txs, Txo = [T(), T(), T(), T()], [T(), T(), T(), T()]
        Dgs = [view(SCR, 26624 + 512 * i_, [128], F32) for i_ in range(4)]
        TDgs = [T() for _ in range(4)]
        for k in range(16):
            Dgk, TDgk = Dgs[k % 4], TDgs[k % 4]
            s.op("dve", lambda k=k, Dgk=Dgk: V.tensor_scalar(out=Dgk, in0=identf[:],
                                                             scalar1=modc[:, l, 32 + k, vec:vec + 1],
                                                             scalar2=None, op0=ALU.mult), [TC, Tmod], [TDgk])
            if k % 4 == 0:
                ps, tp = nbank()
            s.pe([lambda ps=ps, k=k, Dgk=Dgk: AC.matmul(ps[:, (k % 4) * 128:(k % 4 + 1) * 128], onesf[:], Dgk,
                                                        start=True, stop=True)], [TDgk, TC], [tp])
            if k % 4 == 3:
                s.op("act", lambda ps=ps, k=k: SC.copy(out=gbc[:, (k // 4) * 512:(k // 4 + 1) * 512], in_=ps),
                     [tp], [Tg])
        blocks = [wout_d[l][:, b * 512:(b + 1) * 512] for b in range(4)]
        xdst = xsc[name]
        TXD = T("xscr")
        it = 0
        order = [(b_, i_) for b_ in range(4) for i_ in range(NTI)]

        def xload(n):
            if n < len(order):
                b_, i_ = order[n]
                s.dma("sp", gB[n % 4], xs_[n % 4], xsrc[i_ * 128:(i_ + 1) * 128, b_ * 512:(b_ + 1) * 512], [],
                      [Txs[n % 4]])

        xload(0)
        xload(1)
        for b, sl, ts in wf.take(4):
            for i in range(NTI):
                xb, txb = xs_[it % 4], Txs[it % 4]
                ob, tob = xo[it % 4], Txo[it % 4]
                gst = gB[4 + it % 4]
                xload(it + 2)
                it += 1
                ps, tp = nbank()
                s.pe([lambda ps=ps, c=c, i=i, sl=sl: AC.matmul(ps, mixT[:, c, i * 128:(i + 1) * 128], sl[:, c, :],
                                                               start=(c == 0), stop=(c == 15)) for c in range(16)],
                     [Tmix, ts], [tp])
                s.op("dve", lambda ps=ps, b=b: V.tensor_tensor(out=to, in0=ps, in1=gbc[:, b * 512:(b + 1) * 512],
                                                               op=ALU.mult), [tp, Tg], [Tto])
                s.op("dve", lambda xb=xb, ob=ob: V.tensor_tensor(out=ob, in0=to, in1=xb, op=ALU.add),
                     [Tto, txb], [tob])
                s.dma("sp", gst, xdst[i * 128:(i + 1) * 128, b * 512:(b + 1) * 512], ob, [tob], [])
        s.barrier()

    def final(g):
        name, T_ = g["name"], g["T"]
        fng = view(SCR, 0, [2048], F32)
        xt = [view(SCR, 8192, [2048], F32), view(SCR, 16384, [2048], F32)]
        yts = [view(SCR, 24576, [2048], F32), view(R1, 8192, [2048], F32)]
        junk = view(R1, 0, [2048], BF16)
        Tf, Tj = T(), T()
        Txt, Tys, Tsms = [T(), T()], [T(), T()], [T(), T()]
        s.dma("sp", gM, fng, fng_d, [], [Tf])
        nti = T_ // 128

        def fload(i):
            if i < nti:
                s.dma("sp", gB[i % 2], xt[i % 2], xsc[name][i * 128:(i + 1) * 128, :], [], [Txt[i % 2]])

        fload(0)
        for i in range(nti):
            xb, txb = xt[i % 2], Txt[i % 2]
            yt, ty = yts[i % 2], Tys[i % 2]
            c0, tsm = 24 + 2 * (i % 2), Tsms[i % 2]
            fload(i + 1)
            s.op("act", lambda xb=xb, c0=c0: SC.activation(out=junk, in_=xb, func=AF.Square,
                                                           accum_out=small[:, c0:c0 + 1]), [txb], [Tj, tsm])
            s.op("dve", lambda c0=c0: V.tensor_scalar(out=small[:, c0 + 1:c0 + 2], in0=small[:, c0:c0 + 1],
                                                      scalar1=1.0 / D, scalar2=EPS, op0=ALU.mult, op1=ALU.add),
                 [tsm], [tsm])
            s.op("act", lambda c0=c0: SC.activation(out=small[:, c0 + 1:c0 + 2], in_=small[:, c0 + 1:c0 + 2],
                                                    func=AF.Sqrt), [tsm], [tsm])
            s.op("dve", lambda c0=c0: V.reciprocal(out=small[:, c0 + 1:c0 + 2], in_=small[:, c0 + 1:c0 + 2]),
                 [tsm], [tsm])
            s.op("dve", lambda xb=xb, yt=yt, c0=c0: V.scalar_tensor_tensor(
                out=yt, in0=xb, scalar=small[:, c0 + 1:c0 + 2], in1=fng, op0=ALU.mult, op1=ALU.mult),
                 [txb, tsm, Tf], [ty])
            s.dma("sp", gB[4 + i % 2], yout[name][i * 128:(i + 1) * 128, :], yt, [ty], [])
        s.barrier()

    for g in GROUPS:
        for l in range(1 if fast else DEPTH):
            layer(g, l)
        final(g)
    s.barrier()
    return nc


def _consts():
    c = {}
    c["identb"] = np.eye(128, dtype=np.float32).astype(NPBF)
    c["identf"] = np.eye(128, dtype=np.float32)
    for L in (256, 1024):
        N = 2 * L
        t = np.arange(L, dtype=np.float64)[:, None]
        f = np.arange(L, dtype=np.float64)[None, :] + 0.5
        ang = 2.0 * np.pi * f * t / N
        Bm = np.concatenate([np.cos(ang), np.sin(ang)], axis=1)
        Gm = (2.0 / N) * Bm.T
        c["B%d" % L] = np.ascontiguousarray(
            Bm.reshape(L // 128, 128, N).transpose(1, 0, 2)).astype(np.float32).astype(NPBF)
        c["G%d" % L] = np.ascontiguousarray(
            Gm.reshape(N // 128, 128, L).transpose(1, 0, 2)).astype(np.float32).astype(NPBF)
        tpos = np.arange(L, dtype=np.float32)
        t_norm = tpos / np.float32(max(L - 1, 1))
        w = np.float32(2.0 * math.pi) * tpos / np.float32(L)
        bands = np.linspace(1e-4, 15, 16, dtype=np.float32)
        z = np.concatenate([t_norm[:, None], np.cos(w[:, None] * bands), -np.sin(w[:, None] * bands)], axis=-1)
        c["zp%d" % L] = np.ascontiguousarray(z.T.astype(np.float32))
        max_decay = math.log(1e-2) / 0.3
        min_decay = math.log(1e-2) / 1.5
        deltas = np.abs(np.linspace(min_decay, max_decay, 1024, dtype=np.float32))
        wn = (np.exp(-t_norm[:, None] * deltas[None, :]) + np.float32(0.05)).astype(np.float32)
        c["wn%d" % L] = np.ascontiguousarray(wn.reshape(L // 128, 128, 1024).transpose(1, 0, 2))
    tt = np.arange(1024)
    row = (tt // 64).astype(np.float32)
    col = (tt % 64).astype(np.float32)
    inv = (10000.0 ** (-np.arange(32, dtype=np.float32) / 32)).astype(np.float32)
    ang = np.concatenate([row[:, None] * inv, col[:, None] * inv], axis=-1).astype(np.float32)
    cs, sn = np.cos(ang).astype(np.float32), np.sin(ang).astype(np.float32)
    C2 = np.repeat(cs, 2, axis=1)
    c["ropeC"] = np.ascontiguousarray(C2.reshape(8, 128, 128).transpose(1, 0, 2))
    c["ropeS"] = np.ascontiguousarray(sn.reshape(8, 128, 64).transpose(1, 0, 2))
    c["ropeN"] = np.ascontiguousarray((-sn).reshape(8, 128, 64).transpose(1, 0, 2))
    return c


def _col(v):
    v = np.asarray(v, dtype=np.float32)
    sh = v.shape
    n = sh[-1] // 128
    r = v.reshape(sh[:-1] + (n, 128))
    return np.ascontiguousarray(np.moveaxis(r, -1, 0))


_NC = None


def kernel(x_prompt, x_sample, cache_k, cache_v, c, c_ctx, norm_g, w_ada, b_ada, w_in, q_norm_g, k_norm_g,
           conv_w, conv_b, filt_w1, filt_b1, filt_w2, filt_b2, filt_w3, filt_freq, hy_bias, w_out, final_norm_g):
    global _NC
    f = lambda a: np.ascontiguousarray(np.asarray(a, dtype=np.float32))
    x_prompt, x_sample, cache_k, cache_v = f(x_prompt), f(x_sample), f(cache_k), f(cache_v)
    if _NC is None:
        import os
        st = os.environ.get("KSTOP")
        _NC = build_nc(int(st) if st else None, fast=bool(os.environ.get("KFAST")))
    nc = _NC
    cst = _consts()
    shared = dict(cst)
    shared["ng"] = _col(norm_g).reshape(128, 32)
    shared["bada"] = _col(b_ada).reshape(128, 96)
    shared["cw"] = _col(f(conv_w)).reshape(128, 144)
    shared["cb"] = _col(conv_b).reshape(128, 48)
    shared["hb"] = _col(hy_bias).reshape(128, 16)
    fc = np.stack([f(filt_b1), f(filt_b2), f(filt_freq)], axis=-1)
    shared["fcol"] = np.ascontiguousarray(fc.transpose(1, 0, 2)).reshape(64, 6)
    shared["gq"] = np.ascontiguousarray(np.broadcast_to(f(q_norm_g)[None], (128, 2, 128))).reshape(128, 256)
    shared["gk"] = np.ascontiguousarray(np.broadcast_to(f(k_norm_g)[None], (128, 2, 128))).reshape(128, 256)
    shared["fng"] = np.ascontiguousarray(np.broadcast_to(f(final_norm_g)[None], (128, D)))
    import os
    if os.environ.get("KFAST"):
        shared["wada"] = f(w_ada)[:, 0:128]
        shared["win"] = f(w_in)[0:1]
        shared["wout"] = f(w_out)[0:1]
    else:
        shared["wada"] = f(w_ada)
        shared["win"] = f(w_in)
        shared["wout"] = f(w_out)
    shared["fw1"] = f(filt_w1)
    shared["fw2"] = f(filt_w2)
    shared["fw3"] = f(filt_w3)
    cc = f(c)
    cctx = f(c_ctx)
    in_maps = []
    for core in range(8):
        b = core // 4
        m = dict(shared)
        m["xp"] = x_prompt[2 * core:2 * core + 2].reshape(512, D)
        m["xs"] = x_sample[b]
        m["ck"] = cache_k[b].reshape(2, 512, 512)
        m["cv"] = cache_v[b].reshape(2, 512, 512)
        m["ccol"] = np.ascontiguousarray(np.stack([_col(cctx), _col(cc[b])], axis=1)).reshape(128, 32)
        in_maps.append(m)
    res = run_bass_kernel_spmd(nc, in_maps, core_ids=list(range(8)))
    r = res.results
    y_prompt = np.concatenate([r[i]["yp"].reshape(2, 256, D) for i in range(8)], axis=0)
    y_sample = np.stack([r[0]["ys"], r[4]["ys"]], axis=0)
    new_k = np.concatenate([r[i]["nk"].reshape(2, 2, 256, 4, 128) for i in range(8)], axis=0)
    new_v = np.concatenate([r[i]["nv"].reshape(2, 2, 256, 4, 128) for i in range(8)], axis=0)
    return (y_prompt.astype(np.float32), y_sample.astype(np.float32), new_k.astype(np.float32),
            new_v.astype(np.float32))
```
